# Optimizing a Trainium2 kernel written in Bass

```python
import jax, jax.numpy as jnp
from jax import lax
import numpy as np

D_MODEL = 1024
BATCH = 32
SEQ = 2048
DEPTH = 2

GRID_W = 64
CTX_LEN = 256
EPS = 1e-6
POOL_WINDOWS = (2, 4, 8, 16)
POOL_GROUPS = len(POOL_WINDOWS)
POOL_GROUP_DIM = D_MODEL // 8
POOL_WIDTH = POOL_GROUPS * POOL_GROUP_DIM
SGU_HEADS = 4
SGU_HEAD_DIM = D_MODEL // 8
SGU_WIDTH = SGU_HEADS * SGU_HEAD_DIM
CHUNK = 128
AB_IN = 2 * POOL_WIDTH + 3 * SGU_WIDTH
AB_MIX = POOL_WIDTH + SGU_WIDTH
FNET_HEADS = 4
FNET_HEAD_DIM = D_MODEL // 8
FNET_WIDTH = FNET_HEADS * FNET_HEAD_DIM
MLA_HEADS = 8
QK_NOPE = 64
QK_ROPE = 32
V_DIM = 64
Q_LORA = D_MODEL // 4
KV_LORA = D_MODEL // 8
MLA_WIDTH = MLA_HEADS * V_DIM
CD_IN = 2 * FNET_WIDTH + Q_LORA + KV_LORA + QK_ROPE + MLA_WIDTH
CD_MIX = FNET_WIDTH + MLA_WIDTH
ROPE_AXIS = QK_ROPE // 2
ROPE_BASE = 10000.0
Q_BLOCK = 128
ALPHA = (2 * DEPTH) ** 0.25
BETA = (8 * DEPTH) ** -0.25
N_EVEN = (DEPTH + 1) // 2
N_ODD = DEPTH // 2

kernel_name = "hybrid_pool_sgu_fnet_mla_diffusion_block"


def _layer_norm(x, g, b):
    xf = x.astype(jnp.float32)
    mu = jnp.mean(xf, axis=-1, keepdims=True)
    var = jnp.mean(jnp.square(xf - mu), axis=-1, keepdims=True)
    return ((xf - mu) * lax.rsqrt(var + EPS) * g + b).astype(x.dtype)


def _rms_norm(x, g):
    xf = x.astype(jnp.float32)
    y = xf * lax.rsqrt(jnp.mean(jnp.square(xf), axis=-1, keepdims=True) + EPS)
    return (y * g).astype(x.dtype)


def _adaln(cond, w_mod, b_mod):
    m = jax.nn.silu(cond) @ w_mod + b_mod
    return jnp.split(m, 3, axis=-1)


def _post_norm(x, y, gate, g, b):
    return _layer_norm(ALPHA * x + gate * y, g, b)


def _multiscale_pool(a, pool_w, pool_scale):
    bn, L, _ = a.shape
    cs = jnp.cumsum(a.astype(jnp.float32), axis=1)
    cs = jnp.pad(cs, ((0, 0), (1, 0), (0, 0)))
    t = jnp.arange(L)
    outs = []
    for g, w in enumerate(POOL_WINDOWS):
        lo = jnp.clip(t - w // 2, 0, L)
        hi = jnp.clip(t + w - w // 2, 0, L)
        seg = cs[:, :, g * POOL_GROUP_DIM:(g + 1) * POOL_GROUP_DIM]
        s = jnp.take(seg, hi, axis=1) - jnp.take(seg, lo, axis=1)
        outs.append(s / (hi - lo).astype(jnp.float32)[None, :, None])
    pooled = jnp.concatenate(outs, axis=-1).astype(a.dtype)
    d = (pooled - a).reshape(bn, L, POOL_GROUPS, POOL_GROUP_DIM)
    y = jnp.einsum('blgc,gcd->blgd', d, pool_w).reshape(bn, L, POOL_WIDTH)
    return y * pool_scale


def _chunk_sgu(u, v, w_s, b_s):
    bn, L, _ = v.shape
    nc = L // CHUNK
    vh = v.reshape(bn, L, SGU_HEADS, SGU_HEAD_DIM).astype(jnp.float32)
    mu = jnp.mean(vh, axis=-1, keepdims=True)
    var = jnp.mean(jnp.square(vh - mu), axis=-1, keepdims=True)
    vn = ((vh - mu) * lax.rsqrt(var + EPS)).astype(v.dtype)
    vn = vn.reshape(bn, nc, CHUNK, SGU_HEADS, SGU_HEAD_DIM)
    f = jnp.einsum('hij,bnjhc->bnihc', w_s, vn) + jnp.swapaxes(b_s, 0, 1)[:, :, None]
    return u * f.reshape(bn, L, SGU_WIDTH)


def _ab_mixer(h, w_in, pool_w, pool_scale, sgu_w, sgu_b, w_out):
    z = h @ w_in
    a, ga, u, v, gb = jnp.split(z, [POOL_WIDTH, 2 * POOL_WIDTH, 2 * POOL_WIDTH + SGU_WIDTH,
                                    2 * POOL_WIDTH + 2 * SGU_WIDTH], axis=-1)
    ya = _multiscale_pool(a, pool_w, pool_scale) * jax.nn.silu(ga)
    yb = _chunk_sgu(u, v, sgu_w, sgu_b) * jax.nn.silu(gb)
    return jnp.concatenate([ya, yb], axis=-1) @ w_out


def _fourier_mix(f, w_f):
    bn, L, _ = f.shape
    fh = f.reshape(bn, L, FNET_HEADS, FNET_HEAD_DIM).astype(jnp.float32)
    spec = jnp.fft.fft2(fh, axes=(1, 3), norm="ortho").real
    return spec.astype(f.dtype).reshape(bn, L, FNET_WIDTH) @ w_f


def _axial_tables(L, dtype):
    rows = L // GRID_W
    row = jnp.repeat(jnp.arange(rows), GRID_W).astype(jnp.float32)
    col = jnp.tile(jnp.arange(GRID_W), rows).astype(jnp.float32)
    inv = ROPE_BASE ** (-jnp.arange(0, ROPE_AXIS, 2, dtype=jnp.float32) / ROPE_AXIS)
    ang_r = (row[:, None] * inv)[:, None, :]
    ang_c = (col[:, None] * inv)[:, None, :]
    return (jnp.cos(ang_r).astype(dtype), jnp.sin(ang_r).astype(dtype),
            jnp.cos(ang_c).astype(dtype), jnp.sin(ang_c).astype(dtype))


def _rotate(x, cos, sin):
    half = x.shape[-1] // 2
    x1, x2 = x[..., :half], x[..., half:]
    return jnp.concatenate([x1 * cos - x2 * sin, x1 * sin + x2 * cos], axis=-1)


def _axial_rope(x, rope):
    cos_r, sin_r, cos_c, sin_c = rope
    return jnp.concatenate([_rotate(x[..., :ROPE_AXIS], cos_r, sin_r),
                            _rotate(x[..., ROPE_AXIS:], cos_c, sin_c)], axis=-1)


def _split_cd(z):
    o1 = FNET_WIDTH
    o2 = o1 + FNET_WIDTH
    o3 = o2 + Q_LORA
    o4 = o3 + KV_LORA
    o5 = o4 + QK_ROPE
    return jnp.split(z, [o1, o2, o3, o4, o5], axis=-1)


def _mla_q(cq, q_norm, w_q_up, rope):
    bn, L, _ = cq.shape
    q = (_rms_norm(cq, q_norm) @ w_q_up).reshape(bn, L, MLA_HEADS, QK_NOPE + QK_ROPE)
    if rope is not None:
        q = jnp.concatenate([q[..., :QK_NOPE], _axial_rope(q[..., QK_NOPE:], rope)], axis=-1)
    return q


def _mla_kv(ckv, kr, kv_norm, w_kv_up, rope):
    bn, L, _ = ckv.shape
    kv = (_rms_norm(ckv, kv_norm) @ w_kv_up).reshape(bn, L, MLA_HEADS, QK_NOPE + V_DIM)
    k_nope, v = kv[..., :QK_NOPE], kv[..., QK_NOPE:]
    kr = kr[:, :, None, :]
    if rope is not None:
        kr = _axial_rope(kr, rope)
    k = jnp.concatenate([k_nope, jnp.broadcast_to(kr, (bn, L, MLA_HEADS, QK_ROPE))], axis=-1)
    return k, v


def _block_attention(q, k, v):
    bn, L, H, dk = q.shape
    nb = L // Q_BLOCK
    scale = dk ** -0.5
    qb = q.reshape(bn, nb, Q_BLOCK, H, dk).transpose(1, 0, 2, 3, 4)

    def one(qi):
        s = jnp.einsum('bqhd,bkhd->bhqk', qi, k, preferred_element_type=jnp.float32) * scale
        p = jax.nn.softmax(s, axis=-1).astype(v.dtype)
        return jnp.einsum('bhqk,bkhd->bqhd', p, v)

    o = lax.map(one, qb)
    return o.transpose(1, 0, 2, 3, 4).reshape(bn, L, H * V_DIM)


def _cd_out(f_in, f_gate, attn, d_gate, fnet_w, w_out):
    yc = _fourier_mix(f_in, fnet_w) * jax.nn.silu(f_gate)
    yd = attn * jax.nn.silu(d_gate)
    return jnp.concatenate([yc, yd], axis=-1) @ w_out


def setup_inputs(seed: int = 0) -> dict:
    key = jax.random.key(seed)
    ks = jax.random.split(key, 32)

    def nrm(k, shape, scale):
        return jax.random.normal(k, shape, jnp.float32) * scale

    D = D_MODEL
    H = MLA_HEADS
    return {
        "x": nrm(ks[0], (BATCH, SEQ, D), 1.0),
        "c": nrm(ks[1], (BATCH, D), 1.0),
        "ctx": nrm(ks[2], (BATCH, CTX_LEN, D), 1.0),
        "c_ctx": nrm(ks[3], (D,), 1.0),
        "ab_w_mod": nrm(ks[4], (N_EVEN, D, 3 * D), 0.5 * D ** -0.5),
        "ab_b_mod": nrm(ks[5], (N_EVEN, 3 * D), 0.01),
        "ab_w_in": nrm(ks[6], (N_EVEN, D, AB_IN), D ** -0.5),
        "ab_pool_w": nrm(ks[7], (N_EVEN, POOL_GROUPS, POOL_GROUP_DIM, POOL_GROUP_DIM), POOL_GROUP_DIM ** -0.5),
        "ab_pool_scale": 1.0 + nrm(ks[8], (N_EVEN, POOL_WIDTH), 0.1),
        "ab_sgu_w": nrm(ks[9], (N_EVEN, SGU_HEADS, CHUNK, CHUNK), CHUNK ** -0.5),
        "ab_sgu_b": 1.0 + nrm(ks[10], (N_EVEN, SGU_HEADS, CHUNK), 0.01),
        "ab_w_out": nrm(ks[11], (N_EVEN, AB_MIX, D), BETA * AB_MIX ** -0.5),
        "ab_ln_g": 1.0 + nrm(ks[12], (N_EVEN, D), 0.1),
        "ab_ln_b": nrm(ks[13], (N_EVEN, D), 0.01),
        "cd_w_mod": nrm(ks[14], (N_ODD, D, 3 * D), 0.5 * D ** -0.5),
        "cd_b_mod": nrm(ks[15], (N_ODD, 3 * D), 0.01),
        "cd_w_in": nrm(ks[16], (N_ODD, D, CD_IN), D ** -0.5),
        "cd_fnet_w": nrm(ks[17], (N_ODD, FNET_WIDTH, FNET_WIDTH), FNET_WIDTH ** -0.5),
        "cd_q_norm": 1.0 + nrm(ks[18], (N_ODD, Q_LORA), 0.1),
        "cd_kv_norm": 1.0 + nrm(ks[19], (N_ODD, KV_LORA), 0.1),
        "cd_w_q_up": nrm(ks[20], (N_ODD, Q_LORA, H * (QK_NOPE + QK_ROPE)), Q_LORA ** -0.5),
        "cd_w_kv_up": nrm(ks[21], (N_ODD, KV_LORA, H * (QK_NOPE + V_DIM)), KV_LORA ** -0.5),
        "cd_w_out": nrm(ks[22], (N_ODD, CD_MIX, D), BETA * CD_MIX ** -0.5),
        "cd_ln_g": 1.0 + nrm(ks[23], (N_ODD, D), 0.1),
        "cd_ln_b": nrm(ks[24], (N_ODD, D), 0.01),
    }


def reference(x, c, ctx, c_ctx,
              ab_w_mod, ab_b_mod, ab_w_in, ab_pool_w, ab_pool_scale, ab_sgu_w, ab_sgu_b,
              ab_w_out, ab_ln_g, ab_ln_b,
              cd_w_mod, cd_b_mod, cd_w_in, cd_fnet_w, cd_q_norm, cd_kv_norm, cd_w_q_up,
              cd_w_kv_up, cd_w_out, cd_ln_g, cd_ln_b):
    L = x.shape[1]
    rope = _axial_tables(L, x.dtype)
    for i in range(DEPTH):
        last = i == DEPTH - 1
        j = i // 2
        if i % 2 == 0:
            sh, sc, gt = _adaln(c, ab_w_mod[j], ab_b_mod[j])
            h = x * (1.0 + sc[:, None]) + sh[:, None]
            y = _ab_mixer(h, ab_w_in[j], ab_pool_w[j], ab_pool_scale[j], ab_sgu_w[j], ab_sgu_b[j], ab_w_out[j])
            if not last:
                csh, csc, cgt = _adaln(c_ctx, ab_w_mod[j], ab_b_mod[j])
                hc = ctx * (1.0 + csc) + csh
                yc = _ab_mixer(hc, ab_w_in[j], ab_pool_w[j], ab_pool_scale[j], ab_sgu_w[j], ab_sgu_b[j], ab_w_out[j])
                ctx = _post_norm(ctx, yc, cgt, ab_ln_g[j], ab_ln_b[j])
            x = _post_norm(x, y, gt[:, None], ab_ln_g[j], ab_ln_b[j])
        else:
            csh, csc, cgt = _adaln(c_ctx, cd_w_mod[j], cd_b_mod[j])
            hc = ctx * (1.0 + csc) + csh
            fc_in, fc_gate, cq_c, ckv_c, kr_c, dc_gate = _split_cd(hc @ cd_w_in[j])
            k_ctx, v_ctx = _mla_kv(ckv_c, kr_c, cd_kv_norm[j], cd_w_kv_up[j], None)
            sh, sc, gt = _adaln(c, cd_w_mod[j], cd_b_mod[j])
            h = x * (1.0 + sc[:, None]) + sh[:, None]
            f_in, f_gate, cq, ckv, kr, d_gate = _split_cd(h @ cd_w_in[j])
            k_lat, v_lat = _mla_kv(ckv, kr, cd_kv_norm[j], cd_w_kv_up[j], rope)
            q = _mla_q(cq, cd_q_norm[j], cd_w_q_up[j], rope)
            attn = _block_attention(q, jnp.concatenate([k_ctx, k_lat], axis=1),
                                    jnp.concatenate([v_ctx, v_lat], axis=1))
            y = _cd_out(f_in, f_gate, attn, d_gate, cd_fnet_w[j], cd_w_out[j])
            if not last:
                qc = _mla_q(cq_c, cd_q_norm[j], cd_w_q_up[j], None)
                attn_c = _block_attention(qc, k_ctx, v_ctx)
                yc = _cd_out(fc_in, fc_gate, attn_c, dc_gate, cd_fnet_w[j], cd_w_out[j])
                ctx = _post_norm(ctx, yc, cgt, cd_ln_g[j], cd_ln_b[j])
            x = _post_norm(x, y, gt[:, None], cd_ln_g[j], cd_ln_b[j])
    return x
```

```python
import numpy as np
import ml_dtypes
import concourse.bass as bass
import concourse.mybir as mybir
from concourse.bass_utils import run_bass_kernel_spmd

F32 = mybir.dt.float32
BF16 = mybir.dt.bfloat16
AF = mybir.ActivationFunctionType
ALU = mybir.AluOpType

NCORES = 8
NB = 4
L = 2048
LC = 256
D = 1024
NT = L // 128
NTC = LC // 128
ALPHA = 4.0 ** 0.25
EPS = 1e-6
SCALE = 96.0 ** -0.5
NKT = (L + LC) // 128
DEBUG = False


class Buf:
    __slots__ = ("name", "w", "r")

    def __init__(self, name=""):
        self.name = name
        self.w = None
        self.r = {}


class Sched:
    COMPUTE = ("pe", "act", "dve", "pool")
    ALL = ("pe", "act", "dve", "pool", "sp")

    def __init__(self, nc, n_dma_sems=16):
        self.nc = nc
        self.streams = {e: [] for e in self.ALL}
        self.sem = {}
        self.count = {}
        self.seen = {e: {} for e in self.ALL}
        for e in self.COMPUTE:
            self.sem[e] = nc.alloc_semaphore("s_" + e)
            self.count[e] = 0
        self.dma_keys = []
        self.qkeys = {"sp": [], "pool": []}
        for q, nq in (("sp", n_dma_sems), ("pool", 8)):
            for i in range(nq):
                k = "dma_%s%d" % (q, i)
                self.sem[k] = nc.alloc_semaphore("s_" + k)
                self.count[k] = 0
                self.dma_keys.append(k)
                self.qkeys[q].append(k)
        self.dma_rr = {"sp": 0, "pool": 0}

    def _wait(self, eng, key, val):
        if self.seen[eng].get(key, 0) >= val:
            return
        self.seen[eng][key] = val
        self.streams[eng].append(("wait", key, val))

    @staticmethod
    def _deps(reads, writes):
        deps = {}
        for b in reads:
            if b.w is not None:
                k, v = b.w
                if deps.get(k, 0) < v:
                    deps[k] = v
        for b in writes:
            if b.w is not None:
                k, v = b.w
                if deps.get(k, 0) < v:
                    deps[k] = v
            for k, v in b.r.items():
                if deps.get(k, 0) < v:
                    deps[k] = v
        return deps

    def op(self, eng, fn, reads=(), writes=()):
        for k, v in self._deps(reads, writes).items():
            if k == eng and eng == "pe":
                continue
            self._wait(eng, k, v)
        self.count[eng] += 1
        n = self.count[eng]
        self.streams[eng].append(("op", fn, n))
        for b in reads:
            if b.r.get(eng, 0) < n:
                b.r[eng] = n
        for b in writes:
            b.w = (eng, n)
            b.r = {}

    def dma(self, out, in_, reads=(), writes=(), queue="sp"):
        qk = self.qkeys[queue]
        key = qk[self.dma_rr[queue] % len(qk)]
        self.dma_rr[queue] += 1
        if self.count[key] > 0:
            self._wait(queue, key, self.count[key])
        for k, v in self._deps(reads, writes).items():
            self._wait(queue, k, v)
        self.count[key] += 16
        n = self.count[key]
        self.streams[queue].append(("dma", out, in_, key, n))
        for b in reads:
            if b.r.get(key, 0) < n:
                b.r[key] = n
        for b in writes:
            b.w = (key, n)
            b.r = {}

    def barrier(self):
        for e in self.ALL:
            for k in self.dma_keys:
                if self.count[k] > 0:
                    self._wait(e, k, self.count[k])
            for k in self.COMPUTE:
                if k != e and self.count[k] > 0:
                    self._wait(e, k, self.count[k])

    def finish(self):
        for k in self.dma_keys:
            if self.count[k] > 0:
                self._wait("sp", k, self.count[k])
        for e in self.COMPUTE:
            if self.count[e] > 0:
                self._wait("sp", e, self.count[e])

    def replay(self):
        sem = self.sem

        def mk(name):
            items = self.streams[name]

            def body(e):
                for it in items:
                    if it[0] == "wait":
                        e.wait_ge(sem[it[1]], it[2])
                    elif it[0] == "op":
                        it[1](e).then_inc(sem[name], 1)
                    else:
                        e.dma_start(out=it[1], in_=it[2]).then_inc(sem[it[3]], 16)
            return body

        with self.nc.Block() as block:
            block.tensor(mk("pe"))
            block.scalar(mk("act"))
            block.vector(mk("dve"))
            block.gpsimd(mk("pool"))
            block.sync(mk("sp"))


class Arena:
    def __init__(self, nc, name, kib):
        self.n = kib * 256
        self.t = nc.alloc_sbuf_tensor(name, [128, self.n], F32)
        self.off = 0

    def mark(self):
        return self.off

    def reset(self, m):
        self.off = m

    def f32(self, n):
        n4 = (n + 7) // 8 * 8
        assert self.off + n4 <= self.n, ("arena overflow", self.off, n4, self.n)
        ap = self.t[:, self.off:self.off + n]
        self.off += n4
        return ap

    def bf16(self, n):
        w = (n + 1) // 2
        n4 = (w + 7) // 8 * 8
        assert self.off + n4 <= self.n, ("arena overflow", self.off, n4, self.n)
        ap = self.t[:, self.off:self.off + w].bitcast(BF16)
        self.off += n4
        return ap[:, 0:n]


def build_program(stop_after_l0=False):
    nc = bass.Bass("TRN2", target_bir_lowering=False)

    def din(name, shape, dt=F32):
        return nc.dram_tensor(name, list(shape), dt, kind="ExternalInput").ap()

    x_d = din("x", [NB, L, D])
    ctx_d = din("ctx", [NB, LC, D])
    cT_d = din("cT", [128, 8 * 5])
    wmod_d = [din("wmod%d" % l, [D, 3 * D]) for l in range(2)]
    bmod_d = [din("bmod%d" % l, [128, 24 * 5]) for l in range(2)]
    w_in0_d = din("w_in0", [D, 2560])
    w_out0_d = din("w_out0", [D, D])
    w_in1_d = din("w_in1", [D, 1984])
    w_out1_d = din("w_out1", [D, D])
    poolw_d = din("poolw", [128, 4 * 128])
    pscale_d = din("pscale", [128, 4])
    sguw_d = din("sguw", [128, 4 * 128])
    sgub_d = din("sgub", [1, 4 * 512])
    fnet_d = din("fnet", [512, 512])
    qup_d = din("qup", [256, 1024])
    kvup_d = din("kvup", [128, 1024])
    qn_d = din("qn", [128, 2])
    kvn_d = din("kvn", [128, 1])
    lng_d = [din("lng%d" % l, [1, D]) for l in range(2)]
    lnb_d = [din("lnb%d" % l, [1, D]) for l in range(2)]
    ident_d = din("ident", [128, 128])
    band_d = din("band", [128, 20 * 128])
    cs_d = din("cs128", [128, 256])
    rope_d = din("rope", [128, L])
    dft_d = din("dft", [NT, 128, 2 * NT * 128], BF16)
    out_d = nc.dram_tensor("out", [NB, L, D], F32, kind="ExternalOutput").ap()
    x1_d = nc.dram_tensor("x1s", [NB, L, D], F32, kind="Internal").ap()
    c1_d = nc.dram_tensor("c1s", [NB, LC, D], F32, kind="Internal").ap()
    gate_d = nc.dram_tensor("gates", [2 * 5, D], F32, kind="Internal").ap()
    sfg_d = nc.dram_tensor("sfgs", [128, 4, L], BF16, kind="Internal").ap()
    sdg_d = nc.dram_tensor("sdgs", [128, 4, L], BF16, kind="Internal").ap()
    dbg_d = nc.dram_tensor("dbg_mix", [128, 8, L], BF16, kind="Internal").ap() if DEBUG else None
    dbg2_d = nc.dram_tensor("dbg_kv", [128, 4, L + LC], BF16, kind="Internal").ap() if DEBUG else None

    S = Sched(nc)
    A = Arena(nc, "arena", 200)

    pst = [nc.alloc_psum_tensor("psum%d" % i, [128, 1024], F32) for i in range(4)]
    banks = []
    bank_b = []
    for i in range(4):
        for h in range(2):
            banks.append(pst[i][:, h * 512:(h + 1) * 512])
            bank_b.append(Buf("bank%d" % (2 * i + h)))
    pair_ap = [pst[i][:, :] for i in range(4)]
    rr = {"mm": 0, "tr": 0, "big": 0, "o": 0}
    MM_BANKS = [0, 1, 2]
    TR_BANKS = [3]
    BIG_PAIRS = [2, 3]
    O_BANKS = [4, 5, 6, 7]

    def bank(tag):
        lst = {"mm": MM_BANKS, "tr": TR_BANKS, "o": O_BANKS}[tag]
        i = lst[rr[tag] % len(lst)]
        rr[tag] += 1
        return banks[i], bank_b[i]

    def bigpair():
        p = BIG_PAIRS[rr["big"] % 2]
        rr["big"] += 1
        return pair_ap[p], [bank_b[2 * p], bank_b[2 * p + 1]]

    def MM(out, lhsT, rhs, start, stop, r, w):
        S.op("pe", lambda e: e.matmul(out, lhsT, rhs, start=start, stop=stop), reads=r, writes=w)

    def TR(out, in_, ident, r, w):
        S.op("pe", lambda e: e.transpose(out, in_, ident), reads=r, writes=w)

    def ACT(out, in_, func, r, w, bias=None, scale=None, accum=None):
        kw = {}
        if bias is not None:
            kw["bias"] = bias
        if scale is not None:
            kw["scale"] = scale
        if accum is not None:
            kw["accum_out"] = accum
        S.op("act", lambda e: e.activation(out, in_, func, **kw), reads=r, writes=w)

    def TT(eng, out, in0, in1, op, r, w):
        S.op(eng, lambda e: e.tensor_tensor(out, in0, in1, op), reads=r, writes=w)

    def TS(eng, out, in0, s1, s2, op0, op1, r, w):
        if op1 is None:
            S.op(eng, lambda e: e.tensor_scalar(out, in0, s1, None, op0), reads=r, writes=w)
        else:
            S.op(eng, lambda e: e.tensor_scalar(out, in0, s1, s2, op0, op1), reads=r, writes=w)

    def STT(out, in0, scalar, in1, op0, op1, r, w):
        S.op("dve", lambda e: e.scalar_tensor_tensor(out, in0, scalar, in1, op0, op1), reads=r, writes=w)

    def CP(eng, out, in_, r, w):
        if eng == "act":
            S.op(eng, lambda e: e.activation(out, in_, AF.Copy), reads=r, writes=w)
        else:
            S.op(eng, lambda e: e.tensor_copy(out, in_), reads=r, writes=w)

    ident_f = A.f32(128); ident_fb = Buf("ident_f")
    ident_b = A.bf16(128); ident_bb = Buf("ident_b")
    modT = [A.f32(120) for _ in range(2)]
    modb = [Buf("modT0"), Buf("modT1")]
    poolw = A.bf16(512); poolwb = Buf("poolw")
    pscale = A.f32(4); pscaleb = Buf("pscale")
    sguw = A.bf16(512); sguwb = Buf("sguw")
    bs2 = A.bf16(2048); bs2b = Buf("bs2")
    ones2 = A.bf16(128); ones2b = Buf("ones2")
    fnetw = A.bf16(4 * 512); fnetwb = Buf("fnetw")
    qupw = A.bf16(2 * 1024); qupwb = Buf("qupw")
    kvupw = A.bf16(1024); kvupwb = Buf("kvupw")
    band = A.bf16(20 * 128); bandb = Buf("band")
    cs128 = A.bf16(256); cs128b = Buf("cs128")
    PERS = A.mark()

    S.dma(ident_f, ident_d[:, :], writes=[ident_fb])
    S.dma(ident_b, ident_d[:, :], writes=[ident_bb], queue="pool")
    S.dma(poolw, poolw_d[:, :], writes=[poolwb], queue="pool")
    S.dma(pscale, pscale_d[:, :], writes=[pscaleb])
    S.dma(sguw, sguw_d[:, :], writes=[sguwb], queue="pool")
    S.dma(fnetw.rearrange("p (k n) -> p k n", k=4), fnet_d.rearrange("(k p) n -> p k n", p=128),
          writes=[fnetwb], queue="pool")
    S.dma(band, band_d[:, :], writes=[bandb], queue="pool")
    S.dma(cs128, cs_d[:, :], writes=[cs128b], queue="pool")
    S.op("pool", lambda e: e.memset(ones2, 1.0), writes=[ones2b])

    m0 = A.mark()
    tmpq = A.f32(2 * 1024); tmpqb = Buf("tmpq")
    tmpkv = A.f32(1024); tmpkvb = Buf("tmpkv")
    qn = A.f32(2); qnb = Buf("qn")
    kvn = A.f32(1); kvnb = Buf("kvn")
    sgb_f = A.f32(2048); sgb_fb = Buf("sgb_f")
    sgb_hi = A.bf16(2048); sgb_hib = Buf("sgb_hi")
    sgb_hif = A.f32(2048); sgb_hifb = Buf("sgb_hif")
    sgb_lo = A.bf16(2048); sgb_lob = Buf("sgb_lo")
    S.dma(tmpq.rearrange("p (k n) -> p k n", k=2), qup_d.rearrange("(k p) n -> p k n", p=128), writes=[tmpqb])
    S.dma(tmpkv, kvup_d[:, :], writes=[tmpkvb])
    S.dma(qn, qn_d[:, :], writes=[qnb])
    S.dma(kvn, kvn_d[:, :], writes=[kvnb])
    S.dma(sgb_f[0:1, :], sgub_d[:, :], writes=[sgb_fb])
    for k in range(2):
        TS("dve", qupw[:, k * 1024:(k + 1) * 1024], tmpq[:, k * 1024:(k + 1) * 1024], qn[:, k:k + 1], None,
           ALU.mult, None, [tmpqb, qnb], [qupwb])
    TS("dve", kvupw, tmpkv, kvn[:, 0:1], None, ALU.mult, None, [tmpkvb, kvnb], [kvupwb])
    CP("dve", sgb_hi[0:1, :], sgb_f[0:1, :], [sgb_fb], [sgb_hib])
    CP("dve", sgb_hif[0:1, :], sgb_hi[0:1, :], [sgb_hib], [sgb_hifb])
    TT("dve", sgb_lo[0:1, :], sgb_f[0:1, :], sgb_hif[0:1, :], ALU.subtract, [sgb_fb, sgb_hifb], [sgb_lob])
    S.dma(bs2[0:1, :], sgb_hi[0:1, :], reads=[sgb_hib], writes=[bs2b])
    S.dma(bs2[1:2, :], sgb_lo[0:1, :], reads=[sgb_lob], writes=[bs2b])

    cT = A.f32(40); cTb = Buf("cT")
    scT = A.f32(40); scTb = Buf("scT")
    bmod = A.f32(120); bmodb = Buf("bmod")
    wm = A.f32(8 * 3072); wmb = Buf("wm")
    grow = A.f32(D); growb = Buf("grow")
    S.dma(cT, cT_d[:, :], writes=[cTb])
    ACT(scT, cT, AF.Silu, [cTb], [scTb])
    scT3 = scT.rearrange("p (k j) -> p k j", k=8)
    for l in range(2):
        wm3 = wm.rearrange("p (k n) -> p k n", k=8)
        for k in range(8):
            S.dma(wm3[:, k, :], wmod_d[l][k * 128:(k + 1) * 128, :], writes=[wmb])
        S.dma(bmod, bmod_d[l][:, :], writes=[bmodb])
        pb, pbb = bank("mm")
        for j in range(24):
            for k in range(8):
                MM(pb[:, j * 5:(j + 1) * 5], wm3[:, k, j * 128:(j + 1) * 128], scT3[:, k, :],
                   k == 0, k == 7, [wmb, scTb], [pbb])
        TT("dve", modT[l], pb[:, 0:120], bmod, ALU.add, [pbb, bmodb], [modb[l]])
        TS("dve", modT[l][:, 40:80], modT[l][:, 40:80], 1.0, None, ALU.add, None, [modb[l]], [modb[l]])
        for half in range(2):
            tb, tbb = bank("tr")
            for jj in range(4):
                j = 16 + half * 4 + jj
                TR(tb[0:5, jj * 128:(jj + 1) * 128], modT[l][:, j * 5:(j + 1) * 5], ident_f,
                   [modb[l], ident_fb], [tbb])
            CP("dve", grow[0:5, half * 512:(half + 1) * 512], tb[0:5, :], [tbb], [growb])
        S.dma(gate_d[l * 5:(l + 1) * 5, :], grow[0:5, :], reads=[growb])
    gate_db = Buf("gate_d")
    S.barrier()
    A.reset(m0)

    def mod_ap(l, part, k, ci):
        j = part * 8 + k
        return modT[l][:, j * 5 + ci:j * 5 + ci + 1]

    def out_stage(mixT_tile, mixb, wout, woutb, xres, xresb, gbc, gbcb, lng, lngb, lnb, lnbb,
                  work, dst_ap):
        pp, ppb = bigpair()
        for n in range(2):
            for j in range(8):
                MM(pp[:, n * 512:(n + 1) * 512], mixT_tile(j), wout[:, j, n * 512:(n + 1) * 512],
                   j == 0, j == 7, [mixb, woutb], [ppb[n]])
        r = work["r"]; rb = work["rb"]
        TT("dve", r, pp, gbc, ALU.mult, ppb + [gbcb], [rb])
        STT(r, xres, ALPHA, r, ALU.mult, ALU.add, [xresb, rb], [rb])
        st = work["st"]; stb = work["stb"]
        S.op("dve", lambda e: e.bn_stats(st[:, 0:6], r[:, 0:512]), reads=[rb], writes=[stb])
        S.op("dve", lambda e: e.bn_stats(st[:, 6:12], r[:, 512:1024]), reads=[rb], writes=[stb])
        S.op("dve", lambda e: e.bn_aggr(st[:, 12:14], st[:, 0:12]), reads=[stb], writes=[stb])
        ACT(st[:, 14:15], st[:, 13:14], AF.Sqrt, [stb], [stb], bias=EPS, scale=1.0)
        S.op("dve", lambda e: e.reciprocal(st[:, 15:16], st[:, 14:15]), reads=[stb], writes=[stb])
        STT(st[:, 16:17], st[:, 12:13], -1.0, st[:, 15:16], ALU.mult, ALU.mult, [stb], [stb])
        ACT(r, r, AF.Identity, [rb, stb], [rb], bias=st[:, 16:17], scale=st[:, 15:16])
        o = work["o"]; ob = work["ob"]
        TT("pool", o, r, lng, ALU.mult, [rb, lngb], [ob])
        TT("pool", o, o, lnb, ALU.add, [ob, lnbb], [ob])
        S.dma(dst_ap, o, reads=[ob], writes=work.get("dstb", []))

    def load_bcast(dst, src_row, dstb):
        S.dma(dst, src_row.partition_broadcast(128), reads=[gate_db], writes=[dstb])

    for b in range(NB):
        A.reset(PERS)
        w_in = A.bf16(8 * 2560); w_inb = Buf("w_in0")
        w_out = A.bf16(8 * 1024); w_outb = Buf("w_out0")
        w_in3 = w_in.rearrange("p (k n) -> p k n", k=8)
        w_out3 = w_out.rearrange("p (k n) -> p k n", k=8)
        for k in range(8):
            S.dma(w_in3[:, k, :], w_in0_d[k * 128:(k + 1) * 128, :], writes=[w_inb], queue="pool")
        for k in range(8):
            S.dma(w_out3[:, k, :], w_out0_d[k * 128:(k + 1) * 128, :], writes=[w_outb], queue="pool")
        lng = A.f32(D); lngb = Buf("lng"); lnb = A.f32(D); lnbb = Buf("lnb")
        gbc = A.f32(D); gbcb = Buf("gbc")
        S.dma(lng, lng_d[0].partition_broadcast(128), writes=[lngb])
        S.dma(lnb, lnb_d[0].partition_broadcast(128), writes=[lnbb])
        xblk = [A.f32(4 * D) for _ in range(2)]; xblkb = [Buf("xblk0"), Buf("xblk1")]
        hT = A.bf16(8 * 512); hTb = Buf("hT")
        a_tok = A.bf16(NT * 512); a_tokb = [Buf("a_tok%d" % t) for t in range(NT)]
        vn = [A.bf16(4 * 512) for _ in range(2)]; vnb = [Buf("vn0"), Buf("vn1")]
        sga = [A.bf16(4 * 512) for _ in range(2)]; sgab = [Buf("sga0"), Buf("sga1")]
        ug = [A.bf16(4 * 512) for _ in range(2)]; ugb = [Buf("ug0"), Buf("ug1")]
        sgbt = A.bf16(4 * 512); sgbtb = Buf("sgbt")
        dT = A.bf16(4 * 512); dTb = Buf("dT")
        mixT = A.bf16(8 * 512); mixTb = Buf("mixT")
        work = {"r": A.f32(D), "rb": Buf("r"), "st": A.f32(24), "stb": Buf("st"),
                "o": A.f32(D), "ob": Buf("o")}
        vst = A.f32(64); vstb = Buf("vst")
        hT3 = hT.rearrange("p (k n) -> p k n", k=8)
        mixT3 = mixT.rearrange("p (k n) -> p k n", k=8)

        for (src, dst, ntl, ci) in ((ctx_d[b], c1_d[b], NTC, 4), (x_d[b], x1_d[b], NT, b)):
            load_bcast(gbc, gate_d[0 * 5 + ci:0 * 5 + ci + 1, :], gbcb)
            bt = min(4, ntl)
            nblk = ntl // bt
            ntok = bt * 128

            def front(blk):
                s = blk % 2
                xb3 = xblk[s].rearrange("p (t n) -> p t n", t=4)
                for t in range(bt):
                    T = blk * bt + t
                    S.dma(xb3[:, t, :], src[T * 128:(T + 1) * 128, :], writes=[xblkb[s]])
                for k in range(8):
                    tb, tbb = bank("tr")
                    for t in range(bt):
                        TR(tb[:, t * 128:(t + 1) * 128], xb3[:, t, k * 128:(k + 1) * 128], ident_f,
                           [xblkb[s], ident_fb], [tbb])
                    ACT(hT3[:, k, 0:ntok], tb[:, 0:ntok], AF.Identity, [tbb, modb[0]], [hTb],
                        bias=mod_ap(0, 0, k, ci), scale=mod_ap(0, 1, k, ci))
                for t in range(bt):
                    T = blk * bt + t
                    pa, pab = bank("mm")
                    for k in range(8):
                        MM(pa, hT3[:, k, t * 128:(t + 1) * 128], w_in3[:, k, 0:512], k == 0, k == 7,
                           [hTb, w_inb], [pab])
                    CP("dve", a_tok[:, T * 512:(T + 1) * 512], pa, [pab], [a_tokb[T]])
                    pv, pvb = bank("mm")
                    for k in range(8):
                        MM(pv, hT3[:, k, t * 128:(t + 1) * 128], w_in3[:, k, 1536:2048], k == 0, k == 7,
                           [hTb, w_inb], [pvb])
                    for hh in range(4):
                        S.op("dve", lambda e, hh=hh, pv=pv: e.bn_stats(vst[:, hh * 6:(hh + 1) * 6], pv[:, hh * 128:(hh + 1) * 128]),
                             reads=[pvb], writes=[vstb])
                    for hh in range(4):
                        S.op("dve", lambda e, hh=hh: e.bn_aggr(vst[:, 24 + hh * 2:26 + hh * 2], vst[:, hh * 6:(hh + 1) * 6]),
                             reads=[vstb], writes=[vstb])
                    var4 = vst[:, 24:32].rearrange("p (h two) -> p h two", two=2)[:, :, 1]
                    mean4 = vst[:, 24:32].rearrange("p (h two) -> p h two", two=2)[:, :, 0]
                    ACT(vst[:, 32:36], var4, AF.Sqrt, [vstb], [vstb], bias=EPS, scale=1.0)
                    S.op("dve", lambda e: e.reciprocal(vst[:, 36:40], vst[:, 32:36]), reads=[vstb], writes=[vstb])
                    STT(vst[:, 40:44], mean4, -1.0, vst[:, 36:40], ALU.mult, ALU.mult, [vstb], [vstb])
                    for hh in range(4):
                        ACT(vn[s][:, t * 512 + hh * 128:t * 512 + (hh + 1) * 128], pv[:, hh * 128:(hh + 1) * 128],
                            AF.Identity, [pvb, vstb], [vnb[s]],
                            bias=vst[:, 40 + hh:41 + hh], scale=vst[:, 36 + hh:37 + hh])
                for jc in range(4):
                    pg, pgb = bank("mm")
                    for k in range(8):
                        MM(pg[:, 0:ntok], w_in3[:, k, 512 + jc * 128:512 + (jc + 1) * 128], hT3[:, k, 0:ntok],
                           k == 0, k == 7, [hTb, w_inb], [pgb])
                    ACT(sga[s][:, jc * 512:jc * 512 + ntok], pg[:, 0:ntok], AF.Silu, [pgb], [sgab[s]])
                for jc in range(4):
                    pg, pgb = bank("mm")
                    for k in range(8):
                        MM(pg[:, 0:ntok], w_in3[:, k, 2048 + jc * 128:2048 + (jc + 1) * 128], hT3[:, k, 0:ntok],
                           k == 0, k == 7, [hTb, w_inb], [pgb])
                    ACT(sgbt[:, jc * 512:jc * 512 + ntok], pg[:, 0:ntok], AF.Silu, [pgb], [sgbtb])
                    pu, pub = bank("mm")
                    for k in range(8):
                        MM(pu[:, 0:ntok], w_in3[:, k, 1024 + jc * 128:1024 + (jc + 1) * 128], hT3[:, k, 0:ntok],
                           k == 0, k == 7, [hTb, w_inb], [pub])
                    TT("dve", ug[s][:, jc * 512:jc * 512 + ntok], pu[:, 0:ntok], sgbt[:, jc * 512:jc * 512 + ntok],
                       ALU.mult, [pub, sgbtb], [ugb[s]])

            def back(blk):
                s = blk % 2
                xb3 = xblk[s].rearrange("p (t n) -> p t n", t=4)
                for g in range(4):
                    pd, pdb = bank("mm")
                    for t in range(bt):
                        T = blk * bt + t
                        srcs = []
                        if T > 0:
                            srcs.append((T - 1, 0))
                        srcs.append((T, 3 if T == 0 else (4 if T == ntl - 1 else 1)))
                        if T < ntl - 1:
                            srcs.append((T + 1, 2))
                        for i, (Ts, kind) in enumerate(srcs):
                            MM(pd[:, t * 128:(t + 1) * 128],
                               a_tok[:, Ts * 512 + g * 128:Ts * 512 + (g + 1) * 128],
                               band[:, (g * 5 + kind) * 128:(g * 5 + kind + 1) * 128],
                               i == 0, i == len(srcs) - 1, [a_tokb[Ts], bandb], [pdb])
                    CP("act", dT[:, g * 512:g * 512 + ntok], pd[:, 0:ntok], [pdb], [dTb])
                for g in range(4):
                    py, pyb = bank("mm")
                    MM(py[:, 0:ntok], poolw[:, g * 128:(g + 1) * 128], dT[:, g * 512:g * 512 + ntok], True, True,
                       [poolwb, dTb], [pyb])
                    STT(mixT3[:, g, 0:ntok], py[:, 0:ntok], pscale[:, g:g + 1], sga[s][:, g * 512:g * 512 + ntok],
                        ALU.mult, ALU.mult, [pyb, pscaleb, sgab[s]], [mixTb])
                for hh in range(4):
                    pf, pfb = bank("mm")
                    MM(pf[:, 0:ntok], ones2[0:2, :], bs2[0:2, hh * 512:hh * 512 + ntok], True, False,
                       [ones2b, bs2b], [pfb])
                    for t in range(bt):
                        MM(pf[:, t * 128:(t + 1) * 128], vn[s][:, t * 512 + hh * 128:t * 512 + (hh + 1) * 128],
                           sguw[:, hh * 128:(hh + 1) * 128], False, t == bt - 1, [vnb[s], sguwb], [pfb])
                    TT("dve", mixT3[:, 4 + hh, 0:ntok], pf[:, 0:ntok], ug[s][:, hh * 512:hh * 512 + ntok], ALU.mult,
                       [pfb, ugb[s]], [mixTb])
                for t in range(bt):
                    T = blk * bt + t
                    out_stage(lambda j, t=t: mixT3[:, j, t * 128:(t + 1) * 128], mixTb, w_out3, w_outb,
                              xb3[:, t, :], xblkb[s], gbc, gbcb, lng, lngb, lnb, lnbb, work,
                              dst[T * 128:(T + 1) * 128, :])

            for blk in range(nblk + 1):
                if blk < nblk:
                    front(blk)
                if blk >= 1:
                    back(blk - 1)
        S.barrier()
        if stop_after_l0:
            continue
        A.reset(PERS)
        mixL = A.bf16(8 * L); mixLb = Buf("mixL")
        cqnT = A.bf16(2 * L); cqnTb = Buf("cqnT")
        ckvnT = A.bf16(L + LC); ckvnTb = Buf("ckvnT")
        krT = A.bf16(L + LC); krTb = Buf("krT")
        rope = A.f32(L); ropeb = Buf("rope")
        lng = A.f32(D); lngb = Buf("lng1"); lnb = A.f32(D); lnbb = Buf("lnb1")
        gbc = A.f32(D); gbcb = Buf("gbc1")
        mixL3 = mixL.rearrange("p (k n) -> p k n", k=8)
        cqnT3 = cqnT.rearrange("p (k n) -> p k n", k=2)
        S.dma(rope, rope_d[:, :], writes=[ropeb])
        S.dma(lng, lng_d[1].partition_broadcast(128), writes=[lngb])
        S.dma(lnb, lnb_d[1].partition_broadcast(128), writes=[lnbb])
        load_bcast(gbc, gate_d[5 + b:5 + b + 1, :], gbcb)
        PL0 = A.mark()
        G = A.bf16(2 * NT * 512); Gb = Buf("G")
        G4 = G.rearrange("p (c l n) -> p c l n", c=2, l=NT)
        PL1 = A.mark()
        w_in1 = A.bf16(8 * 1984); w_in1b = Buf("w_in1")
        w_in13 = w_in1.rearrange("p (k n) -> p k n", k=8)
        for k in range(8):
            S.dma(w_in13[:, k, :], w_in1_d[k * 128:(k + 1) * 128, :], writes=[w_in1b], queue="pool")
        xblk1 = A.f32(4 * D); xblk1b = Buf("xblk1")
        h1T = A.bf16(8 * 512); h1Tb = Buf("h1T")
        h1T3 = h1T.rearrange("p (k n) -> p k n", k=8)
        fT = A.bf16(4 * 512); fTb = Buf("fT")
        gst = [A.bf16(4 * 512) for _ in range(2)]; gstb = [Buf("gst0"), Buf("gst1")]
        nrm = A.bf16(384); nrmb = Buf("nrm")
        krf = A.f32(64); krfb = Buf("krf")
        rp1 = A.f32(128); rp1b = Buf("rp1"); rp2 = A.f32(128); rp2b = Buf("rp2")
        junk = A.bf16(256); junkb = Buf("junk")
        tst = A.f32(8); tstb = Buf("tst")
        xb13 = xblk1.rearrange("p (t n) -> p t n", t=4)

        for (src, ntl, ci, is_ctx, key0) in ((c1_d[b], NTC, 4, True, 0), (x1_d[b], NT, b, False, LC)):
            bt = min(4, ntl)
            nblk = ntl // bt
            ntok = bt * 128
            for blk in range(nblk):
                for t in range(bt):
                    T = blk * bt + t
                    S.dma(xb13[:, t, :], src[T * 128:(T + 1) * 128, :], writes=[xblk1b])
                for k in range(8):
                    tb, tbb = bank("tr")
                    for t in range(bt):
                        TR(tb[:, t * 128:(t + 1) * 128], xb13[:, t, k * 128:(k + 1) * 128], ident_f,
                           [xblk1b, ident_fb], [tbb])
                    ACT(h1T3[:, k, 0:ntok], tb[:, 0:ntok], AF.Identity, [tbb, modb[1]], [h1Tb],
                        bias=mod_ap(1, 0, k, ci), scale=mod_ap(1, 1, k, ci))
                if not is_ctx:
                    for jc in range(4):
                        pg, pgb = bank("mm")
                        for k in range(8):
                            MM(pg, w_in13[:, k, jc * 128:(jc + 1) * 128], h1T3[:, k, :], k == 0, k == 7,
                               [h1Tb, w_in1b], [pgb])
                        CP("dve", fT[:, jc * 512:(jc + 1) * 512], pg, [pgb], [fTb])
                    for gi, (c0, dstg) in enumerate(((512, sfg_d), (1024, sdg_d))):
                        for jc in range(4):
                            pg, pgb = bank("mm")
                            for k in range(8):
                                MM(pg, w_in13[:, k, c0 + jc * 128:c0 + (jc + 1) * 128], h1T3[:, k, :], k == 0, k == 7,
                                   [h1Tb, w_in1b], [pgb])
                            ACT(gst[gi][:, jc * 512:(jc + 1) * 512], pg, AF.Silu, [pgb], [gstb[gi]])
                        S.dma(dstg[:, :, blk * 512:(blk + 1) * 512], gst[gi].rearrange("p (k n) -> p k n", k=4),
                              reads=[gstb[gi]])
                    for t in range(bt):
                        T = blk * bt + t
                        for c in range(2):
                            pg, pgb = bank("mm")
                            for hh in range(4):
                                MM(pg[:, hh * 128:(hh + 1) * 128], fT[:, hh * 512 + t * 128:hh * 512 + (t + 1) * 128],
                                   cs128[:, c * 128:(c + 1) * 128], True, True, [fTb, cs128b], [pgb])
                            CP("act" if c == 0 else "dve", G4[:, c, T, :], pg, [pgb], [Gb])
                c0, c1 = (1792, 1952) if is_ctx else (1536, 1984)
                okv = 0 if is_ctx else 256
                for t in range(bt):
                    T = blk * bt + t
                    pt_, ptb = bank("mm")
                    for k in range(8):
                        MM(pt_[:, 0:c1 - c0], h1T3[:, k, t * 128:(t + 1) * 128], w_in13[:, k, c0:c1], k == 0, k == 7,
                           [h1Tb, w_in1b], [ptb])
                    ACT(junk[:, 0:128], pt_[:, okv:okv + 128], AF.Square, [ptb], [junkb, tstb], accum=tst[:, 1:2])
                    ACT(tst[:, 3:4], tst[:, 1:2], AF.Sqrt, [tstb], [tstb], bias=EPS, scale=1.0 / 128)
                    if not is_ctx:
                        ACT(junk[:, 0:256], pt_[:, 0:256], AF.Square, [ptb], [junkb, tstb], accum=tst[:, 0:1])
                        ACT(tst[:, 2:3], tst[:, 0:1], AF.Sqrt, [tstb], [tstb], bias=EPS, scale=1.0 / 256)
                        S.op("dve", lambda e: e.reciprocal(tst[:, 4:6], tst[:, 2:4]), reads=[tstb], writes=[tstb])
                        TS("dve", nrm[:, 0:256], pt_[:, 0:256], tst[:, 4:5], None, ALU.mult, None, [ptb, tstb], [nrmb])
                    else:
                        S.op("dve", lambda e: e.reciprocal(tst[:, 5:6], tst[:, 3:4]), reads=[tstb], writes=[tstb])
                    TS("dve", nrm[:, 256:384], pt_[:, okv:okv + 128], tst[:, 5:6], None, ALU.mult, None,
                       [ptb, tstb], [nrmb])
                    nkr = 32 if is_ctx else 64
                    CP("dve", krf[:, 0:nkr], pt_[:, okv + 128:okv + 128 + nkr], [ptb], [krfb])
                    tb, tbb = bank("tr")
                    tb16 = tb.bitcast(BF16)
                    j0 = 2 if is_ctx else 0
                    for j in range(j0, 3):
                        TR(tb16[:, j * 128:(j + 1) * 128], nrm[:, j * 128:(j + 1) * 128], ident_b,
                           [nrmb, ident_bb], [tbb])
                    if not is_ctx:
                        CP("act", cqnT3[:, :, T * 128:(T + 1) * 128],
                           tb16[:, 0:256].rearrange("p (k n) -> p k n", k=2), [tbb], [cqnTb])
                    CP("act", ckvnT[:, key0 + T * 128:key0 + (T + 1) * 128], tb16[:, 256:384], [tbb], [ckvnTb])
                    tb2, tb2b = bank("tr")
                    TR(tb2[0:nkr, 0:128], krf[:, 0:nkr], ident_f, [krfb, ident_fb], [tb2b])
                    kcol = krT[0:32, key0 + T * 128:key0 + (T + 1) * 128]
                    if is_ctx:
                        CP("dve", kcol, tb2[0:32, 0:128], [tb2b], [krTb])
                    else:
                        TT("dve", rp1[0:32, :], tb2[0:32, 0:128], rope[0:32, T * 128:(T + 1) * 128], ALU.mult,
                           [tb2b, ropeb], [rp1b])
                        TT("dve", rp2[0:32, :], tb2[32:64, 0:128], rope[32:64, T * 128:(T + 1) * 128], ALU.mult,
                           [tb2b, ropeb], [rp2b])
                        TT("pool", kcol, rp1[0:32, :], rp2[0:32, :], ALU.add, [rp1b, rp2b], [krTb])

        S.barrier()
        A.reset(PL1)
        tab = [A.bf16(2 * NT * 128) for _ in range(2)]; tabb = [Buf("tab0"), Buf("tab1")]
        spec_sb = A.bf16(512); spec_sbb = Buf("spec_sb")
        specT = A.bf16(4 * 512); specTb = Buf("specT")
        sfgk = A.bf16(4 * 512); sfgkb = Buf("sfgk")
        specT3 = specT.rearrange("p (h n) -> p h n", h=4)
        sfgk3 = sfgk.rearrange("p (h n) -> p h n", h=4)
        fnetw3 = fnetw.rearrange("p (k n) -> p k n", k=4)
        for blk in range(4):
            S.dma(sfgk3, sfg_d[:, :, blk * 512:(blk + 1) * 512], writes=[sfgkb])
            for q4 in range(4):
                kt = blk * 4 + q4
                tbuf = tab[kt % 2]
                S.dma(tbuf, dft_d[kt], writes=[tabb[kt % 2]])
                tab4 = tbuf.rearrange("p (c l k) -> p c l k", c=2, l=NT)
                ps_, psb_ = bank("mm")
                i = 0
                for c in range(2):
                    for lt in range(NT):
                        MM(ps_, tab4[:, c, lt, :], G4[:, c, lt, :], i == 0, i == 2 * NT - 1,
                           [tabb[kt % 2], Gb], [psb_])
                        i += 1
                CP("act", spec_sb, ps_, [psb_], [spec_sbb])
                tb, tbb = bank("tr")
                tb16 = tb.bitcast(BF16)
                for hh in range(4):
                    TR(tb16[:, hh * 128:(hh + 1) * 128], spec_sb[:, hh * 128:(hh + 1) * 128], ident_b,
                       [spec_sbb, ident_bb], [tbb])
                CP("dve", specT3[:, :, q4 * 128:(q4 + 1) * 128],
                   tb16[:, 0:512].rearrange("p (h n) -> p h n", h=4), [tbb], [specTb])
            for j in range(4):
                pf, pfb = bank("mm")
                for hh in range(4):
                    MM(pf, fnetw3[:, hh, j * 128:(j + 1) * 128], specT3[:, hh, :], hh == 0, hh == 3,
                       [fnetwb, specTb], [pfb])
                TT("dve", mixL3[:, j, blk * 512:(blk + 1) * 512], pf, sfgk3[:, j, :], ALU.mult,
                   [pfb, sfgkb], [mixLb])

        S.barrier()
        A.reset(PL0)
        Vp = A.bf16(NKT * 768); Vpb = Buf("Vp")
        Vp4 = Vp.rearrange("p (t g n) -> p t g n", t=NKT, g=4)
        kT = [A.bf16(L + LC) for _ in range(2)]; kTb = [Buf("kT0"), Buf("kT1")]
        qT = [A.bf16(L) for _ in range(2)]; qTb = [Buf("qT0"), Buf("qT1")]
        PT = [A.bf16(512) for _ in range(3)]; PTb = [Buf("PT%d" % i) for i in range(3)]
        tmp1 = A.f32(512); tmp1b = Buf("tmp1"); tmp2 = A.f32(512); tmp2b = Buf("tmp2")
        rec = [A.f32(512) for _ in range(2)]; recb = [Buf("rec0"), Buf("rec1")]
        sdgk = [A.bf16(512) for _ in range(2)]; sdgkb = [Buf("sdgk0"), Buf("sdgk1")]
        qupw3 = qupw.rearrange("p (k n) -> p k n", k=2)
        S.op("pool", lambda e, o_=Vp: e.memset(o_, 1.0), writes=[Vpb])
        for kt in range(NKT):
            pv, pvb = bank("mm")
            MM(pv, ckvnT[:, kt * 128:(kt + 1) * 128], kvupw[:, 512:1024], True, True, [ckvnTb, kvupwb], [pvb])
            pv4 = pv.rearrange("p (g two n) -> p g two n", g=4, two=2)
            CP("dve", Vp4[:, kt, :, 0:64], pv4[:, :, 0, :], [pvb], [Vpb])
            CP("act", Vp4[:, kt, :, 128:192], pv4[:, :, 1, :], [pvb], [Vpb])
        nrec = 0
        for hh in range(8):
            s = hh % 2
            pr = hh // 2
            for (c0, n) in ((0, 512), (512, 512), (1024, 512), (1536, 512), (2048, 256)):
                pk, pkb = bank("mm")
                MM(pk[0:64, 0:n], kvupw[:, hh * 64:(hh + 1) * 64], ckvnT[:, c0:c0 + n], True, True,
                   [kvupwb, ckvnTb], [pkb])
                CP("dve", kT[s][0:64, c0:c0 + n], pk[0:64, 0:n], [pkb], [kTb[s]])
            CP("pool", kT[s][64:96, :], krT[0:32, :], [krTb], [kTb[s]])
            for qb in range(4):
                pq, pqb = bank("o")
                pp_, ppb_ = bank("o")
                for kc in range(2):
                    MM(pq[0:96, :], qupw3[:, kc, hh * 96:(hh + 1) * 96], cqnT3[:, kc, qb * 512:(qb + 1) * 512],
                       kc == 0, kc == 1, [qupwb, cqnTb], [pqb])
                for kc in range(2):
                    MM(pp_[0:32, :], qupw3[:, kc, 768 + hh * 32:768 + (hh + 1) * 32],
                       cqnT3[:, kc, qb * 512:(qb + 1) * 512], kc == 0, kc == 1, [qupwb, cqnTb], [ppb_])
                CP("act", qT[s][0:64, qb * 512:(qb + 1) * 512], pq[0:64, :], [pqb], [qTb[s]])
                TT("dve", tmp1[64:96, :], pq[64:96, :], rope[64:96, qb * 512:(qb + 1) * 512], ALU.mult,
                   [pqb, ropeb], [tmp1b])
                TT("dve", tmp2[64:96, :], pp_[0:32, :], rope[32:64, qb * 512:(qb + 1) * 512], ALU.mult,
                   [ppb_, ropeb], [tmp2b])
                TT("pool", qT[s][64:96, qb * 512:(qb + 1) * 512], tmp1[64:96, :], tmp2[64:96, :], ALU.add,
                   [tmp1b, tmp2b], [qTb[s]])
            a0 = (hh % 2) * 64
            b0 = 64 - a0
            for qb in range(4):
                po, pob = bank("o")
                pend = None
                npt = 0

                def score(kt):
                    psc, pscb = bank("mm")
                    MM(psc, kT[s][0:96, kt * 128:(kt + 1) * 128], qT[s][0:96, qb * 512:(qb + 1) * 512], True, True,
                       [kTb[s], qTb[s]], [pscb])
                    return psc, pscb

                nxt = score(0)
                for kt in range(NKT):
                    psc, pscb = nxt
                    pi = (hh * 4 * NKT + qb * NKT + kt) % 3
                    ACT(PT[pi], psc, AF.Exp, [pscb], [PTb[pi]], scale=SCALE)
                    if kt + 1 < NKT:
                        nxt = score(kt + 1)
                    lv = Vp4[:, kt, pr, 0:128] if a0 == 0 else Vp4[:, kt, pr, 64:192]
                    MM(po, lv, PT[pi], kt == 0, kt == NKT - 1, [Vpb, PTb[pi]], [pob])
                ri = nrec % 2
                nrec += 1
                S.op("dve", lambda e, o_=rec[ri][a0:a0 + 64, :], i_=po[b0:b0 + 64, :]: e.reciprocal(o_, i_),
                     reads=[pob], writes=[recb[ri]])
                S.dma(sdgk[ri][a0:a0 + 64, :], sdg_d[a0:a0 + 64, pr, qb * 512:(qb + 1) * 512], writes=[sdgkb[ri]])
                TT("pool", rec[ri][a0:a0 + 64, :], rec[ri][a0:a0 + 64, :], sdgk[ri][a0:a0 + 64, :], ALU.mult,
                   [recb[ri], sdgkb[ri]], [recb[ri]])
                TT("dve", mixL3[a0:a0 + 64, 4 + pr, qb * 512:(qb + 1) * 512], po[a0:a0 + 64, :],
                   rec[ri][a0:a0 + 64, :], ALU.mult, [pob, recb[ri]], [mixLb])

        S.barrier()
        if DEBUG and b == 0:
            S.dma(dbg_d, mixL3, reads=[mixLb])
            S.dma(dbg2_d[:, 0, :], ckvnT, reads=[ckvnTb])
            S.dma(dbg2_d[:, 1, :], krT, reads=[krTb])
            S.dma(dbg2_d[:, 2, 0:L], cqnT3[:, 0, :], reads=[cqnTb])
            S.dma(dbg2_d[:, 3, 0:L], cqnT3[:, 1, :], reads=[cqnTb])
            S.barrier()
        A.reset(PL0)
        w_out1 = A.bf16(8 * 1024); w_out1b = Buf("w_out1")
        w_out13 = w_out1.rearrange("p (k n) -> p k n", k=8)
        for k in range(8):
            S.dma(w_out13[:, k, :], w_out1_d[k * 128:(k + 1) * 128, :], writes=[w_out1b], queue="pool")
        xt = [A.f32(D) for _ in range(2)]; xtb = [Buf("xt0"), Buf("xt1")]
        works = [{"r": A.f32(D), "rb": Buf("r"), "st": A.f32(24), "stb": Buf("st"),
                  "o": A.f32(D), "ob": Buf("o")} for _ in range(2)]
        for T in range(NT):
            S.dma(xt[T % 2], x1_d[b][T * 128:(T + 1) * 128, :], writes=[xtb[T % 2]])
            out_stage(lambda j, T=T: mixL3[:, j, T * 128:(T + 1) * 128], mixLb, w_out13, w_out1b,
                      xt[T % 2], xtb[T % 2], gbc, gbcb, lng, lngb, lnb, lnbb, works[T % 2],
                      out_d[b][T * 128:(T + 1) * 128, :])
        S.barrier()

    if stop_after_l0:
        A.reset(PERS)
        tb_ = A.f32(D); tbb_ = Buf("cpy")
        for b in range(NB):
            for T in range(NT):
                S.dma(tb_, x1_d[b][T * 128:(T + 1) * 128, :], writes=[tbb_])
                S.dma(out_d[b][T * 128:(T + 1) * 128, :], tb_, reads=[tbb_])

    S.finish()
    S.replay()
    return nc, S


def _band_mats():
    Ls = 384
    out = np.zeros((4, 5, 128, 128), np.float32)
    for g, w in enumerate((2, 4, 8, 16)):
        P = np.zeros((Ls, Ls), np.float64)
        for t in range(Ls):
            lo = max(t - w // 2, 0)
            hi = min(t + w - w // 2, Ls)
            P[t, lo:hi] = 1.0 / (hi - lo)
        M = P - np.eye(Ls)
        blk = lambda to, ti: M[to * 128:(to + 1) * 128, ti * 128:(ti + 1) * 128].T
        out[g, 0] = blk(1, 0)
        out[g, 1] = blk(1, 1)
        out[g, 2] = blk(1, 2)
        out[g, 3] = blk(0, 0)
        out[g, 4] = blk(2, 2)
    return np.ascontiguousarray(out.transpose(2, 0, 1, 3).reshape(128, 20 * 128))


def _perm32():
    p = np.zeros(32, np.int64)
    for d in range(32):
        base = (d // 16) * 16
        p[d] = base + ((d % 16) + 8) % 16
    return p


def _consts():
    c = {}
    c["ident"] = np.eye(128, dtype=np.float32)
    c["band"] = _band_mats()
    cc = np.arange(128, dtype=np.float64)
    ang = 2.0 * np.pi * np.outer(cc, cc) / 128.0
    c["cs128"] = np.concatenate([np.cos(ang), np.sin(ang)], axis=1).astype(np.float32) / np.float32(np.sqrt(128.0))
    pos = np.arange(L)
    row = (pos // 64).astype(np.float32)
    col = (pos % 64).astype(np.float32)
    inv = (np.float32(10000.0) ** (-np.arange(0, 16, 2, dtype=np.float32) / np.float32(16.0))).astype(np.float32)
    ang_r = (row[:, None] * inv[None, :]).astype(np.float32).astype(np.float64)
    ang_c = (col[:, None] * inv[None, :]).astype(np.float32).astype(np.float64)
    cosT = np.zeros((32, L)); sinT = np.zeros((32, L))
    for d in range(32):
        a = ang_r if d < 16 else ang_c
        i8 = (d % 16) % 8
        cosT[d] = np.cos(a[:, i8])
        sgn = -1.0 if (d % 16) < 8 else 1.0
        sinT[d] = sgn * np.sin(a[:, i8])
    c["rope"] = np.concatenate([cosT, sinT, cosT, sinT], axis=0).astype(np.float32)
    l_idx = (np.arange(NT)[None, :] * 128 + np.arange(128)[:, None])
    k_idx = np.arange(L).reshape(NT, 128)
    prod = (l_idx[None, :, :, None].astype(np.int64) * k_idx[:, None, None, :].astype(np.int64)) % L
    th = 2.0 * np.pi * prod.astype(np.float64) / L
    tab = np.stack([np.cos(th), -np.sin(th)], axis=2) / np.sqrt(float(L))
    c["dft"] = np.ascontiguousarray(tab.reshape(NT, 128, 2 * NT * 128).astype(np.float32)).astype(ml_dtypes.bfloat16)
    return c


_CONSTS = None


def _get_consts():
    global _CONSTS
    if _CONSTS is None:
        _CONSTS = _consts()
    return _CONSTS


def _host_maps(inp):
    f = lambda a: np.ascontiguousarray(np.asarray(a, dtype=np.float32))
    cst = _get_consts()
    shared = dict(cst)
    for l, pre in enumerate(("ab", "cd")):
        shared["wmod%d" % l] = f(inp[pre + "_w_mod"][0])
        bm = f(inp[pre + "_b_mod"][0]).reshape(24, 128).T
        shared["bmod%d" % l] = f(np.repeat(bm[:, :, None], 5, axis=2).reshape(128, 120))
        shared["lng%d" % l] = f(inp[pre + "_ln_g"][0]).reshape(1, D)
        shared["lnb%d" % l] = f(inp[pre + "_ln_b"][0]).reshape(1, D)
    shared["w_in0"] = f(inp["ab_w_in"][0])
    shared["w_out0"] = f(inp["ab_w_out"][0])
    w1 = f(inp["cd_w_in"][0])
    p32 = _perm32()
    kr = w1[:, 1408:1440]
    shared["w_in1"] = f(np.concatenate([w1[:, 0:512], w1[:, 512:1024], w1[:, 1440:1952], w1[:, 1024:1280],
                                        w1[:, 1280:1408], kr, kr[:, p32]], axis=1))
    shared["w_out1"] = f(inp["cd_w_out"][0])
    shared["poolw"] = f(f(inp["ab_pool_w"][0]).transpose(1, 0, 2).reshape(128, 512))
    shared["pscale"] = f(f(inp["ab_pool_scale"][0]).reshape(4, 128).T)
    shared["sguw"] = f(f(inp["ab_sgu_w"][0]).transpose(2, 0, 1).reshape(128, 512))
    sb = f(inp["ab_sgu_b"][0])
    shared["sgub"] = f(np.repeat(sb[:, None, :], 4, axis=1).reshape(1, 2048))
    shared["fnet"] = f(inp["cd_fnet_w"][0])
    qu = f(inp["cd_w_q_up"][0]).reshape(256, 8, 96)
    shared["qup"] = f(np.concatenate([qu.reshape(256, 768), qu[:, :, 64:96][:, :, p32].reshape(256, 256)], axis=1))
    kv = f(inp["cd_w_kv_up"][0]).reshape(128, 8, 128)
    shared["kvup"] = f(np.concatenate([kv[:, :, 0:64].reshape(128, 512), kv[:, :, 64:128].reshape(128, 512)], axis=1))
    shared["qn"] = f(f(inp["cd_q_norm"][0]).reshape(2, 128).T)
    shared["kvn"] = f(f(inp["cd_kv_norm"][0]).reshape(128, 1))
    x = np.asarray(inp["x"], dtype=np.float32)
    ctx = np.asarray(inp["ctx"], dtype=np.float32)
    c = f(inp["c"])
    c_ctx = f(inp["c_ctx"]).reshape(1, D)
    maps = []
    for i in range(NCORES):
        m = dict(shared)
        m["x"] = np.ascontiguousarray(x[i * NB:(i + 1) * NB])
        m["ctx"] = np.ascontiguousarray(ctx[i * NB:(i + 1) * NB])
        cond = np.concatenate([c[i * NB:(i + 1) * NB], c_ctx], axis=0)
        m["cT"] = f(cond.reshape(5, 8, 128).transpose(2, 1, 0).reshape(128, 40))
        maps.append(m)
    return maps


_PROG = {}


def _get_prog(stop_after_l0=False):
    if stop_after_l0 not in _PROG:
        _PROG[stop_after_l0] = build_program(stop_after_l0)[0]
    return _PROG[stop_after_l0]


def kernel(**inp):
    maps = _host_maps(inp)
    nc = _get_prog(False)
    res = run_bass_kernel_spmd(nc, maps, core_ids=list(range(NCORES)))
    return np.concatenate([np.asarray(r["out"], dtype=np.float32) for r in res.results], axis=0)
```

```python
import numpy as np
import ml_dtypes
import concourse.bass as bass
import concourse.mybir as mybir
from concourse.bass_utils import run_bass_kernel_spmd

F32 = mybir.dt.float32
BF16 = mybir.dt.bfloat16
AF = mybir.ActivationFunctionType
ALU = mybir.AluOpType

NCORES = 8
NB = 4
L = 2048
LC = 256
D = 1024
NT = L // 128
NTC = LC // 128
ALPHA = 4.0 ** 0.25
EPS = 1e-6
SCALE = 96.0 ** -0.5
NKT = (L + LC) // 128
DEBUG = False


class Buf:
    __slots__ = ("name", "w", "r")

    def __init__(self, name=""):
        self.name = name
        self.w = None
        self.r = {}


class Sched:
    COMPUTE = ("pe", "act", "dve", "pool")
    ALL = ("pe", "act", "dve", "pool", "sp")

    def __init__(self, nc, n_dma_sems=16):
        self.nc = nc
        self.streams = {e: [] for e in self.ALL}
        self.sem = {}
        self.count = {}
        self.seen = {e: {} for e in self.ALL}
        for e in self.COMPUTE:
            self.sem[e] = nc.alloc_semaphore("s_" + e)
            self.count[e] = 0
        self.dma_keys = []
        self.qkeys = {"sp": [], "pool": []}
        for q, nq in (("sp", n_dma_sems), ("pool", 8)):
            for i in range(nq):
                k = "dma_%s%d" % (q, i)
                self.sem[k] = nc.alloc_semaphore("s_" + k)
                self.count[k] = 0
                self.dma_keys.append(k)
                self.qkeys[q].append(k)
        self.dma_rr = {"sp": 0, "pool": 0}

    def _wait(self, eng, key, val):
        if self.seen[eng].get(key, 0) >= val:
            return
        self.seen[eng][key] = val
        self.streams[eng].append(("wait", key, val))

    @staticmethod
    def _deps(reads, writes):
        deps = {}
        for b in reads:
            if b.w is not None:
                k, v = b.w
                if deps.get(k, 0) < v:
                    deps[k] = v
        for b in writes:
            if b.w is not None:
                k, v = b.w
                if deps.get(k, 0) < v:
                    deps[k] = v
            for k, v in b.r.items():
                if deps.get(k, 0) < v:
                    deps[k] = v
        return deps

    def op(self, eng, fn, reads=(), writes=()):
        for k, v in self._deps(reads, writes).items():
            if k == eng and eng == "pe":
                continue
            self._wait(eng, k, v)
        self.count[eng] += 1
        n = self.count[eng]
        self.streams[eng].append(("op", fn, n))
        for b in reads:
            if b.r.get(eng, 0) < n:
                b.r[eng] = n
        for b in writes:
            b.w = (eng, n)
            b.r = {}

    def dma(self, out, in_, reads=(), writes=(), queue="sp"):
        qk = self.qkeys[queue]
        key = qk[self.dma_rr[queue] % len(qk)]
        self.dma_rr[queue] += 1
        if self.count[key] > 0:
            self._wait(queue, key, self.count[key])
        for k, v in self._deps(reads, writes).items():
            self._wait(queue, k, v)
        self.count[key] += 16
        n = self.count[key]
        self.streams[queue].append(("dma", out, in_, key, n))
        for b in reads:
            if b.r.get(key, 0) < n:
                b.r[key] = n
        for b in writes:
            b.w = (key, n)
            b.r = {}

    def barrier(self):
        for e in self.ALL:
            for k in self.dma_keys:
                if self.count[k] > 0:
                    self._wait(e, k, self.count[k])
            for k in self.COMPUTE:
                if k != e and self.count[k] > 0:
                    self._wait(e, k, self.count[k])

    def finish(self):
        for k in self.dma_keys:
            if self.count[k] > 0:
                self._wait("sp", k, self.count[k])
        for e in self.COMPUTE:
            if self.count[e] > 0:
                self._wait("sp", e, self.count[e])

    def replay(self):
        sem = self.sem

        def mk(name):
            items = self.streams[name]

            def body(e):
                for it in items:
                    if it[0] == "wait":
                        e.wait_ge(sem[it[1]], it[2])
                    elif it[0] == "op":
                        it[1](e).then_inc(sem[name], 1)
                    else:
                        e.dma_start(out=it[1], in_=it[2]).then_inc(sem[it[3]], 16)
            return body

        with self.nc.Block() as block:
            block.tensor(mk("pe"))
            block.scalar(mk("act"))
            block.vector(mk("dve"))
            block.gpsimd(mk("pool"))
            block.sync(mk("sp"))


class Arena:
    def __init__(self, nc, name, kib):
        self.n = kib * 256
        self.t = nc.alloc_sbuf_tensor(name, [128, self.n], F32)
        self.off = 0

    def mark(self):
        return self.off

    def reset(self, m):
        self.off = m

    def f32(self, n):
        n4 = (n + 7) // 8 * 8
        assert self.off + n4 <= self.n, ("arena overflow", self.off, n4, self.n)
        ap = self.t[:, self.off:self.off + n]
        self.off += n4
        return ap

    def bf16(self, n):
        w = (n + 1) // 2
        n4 = (w + 7) // 8 * 8
        assert self.off + n4 <= self.n, ("arena overflow", self.off, n4, self.n)
        ap = self.t[:, self.off:self.off + w].bitcast(BF16)
        self.off += n4
        return ap[:, 0:n]


def build_program(stop_after_l0=False):
    nc = bass.Bass("TRN2", target_bir_lowering=False)

    def din(name, shape, dt=F32):
        return nc.dram_tensor(name, list(shape), dt, kind="ExternalInput").ap()

    x_d = din("x", [NB, L, D])
    ctx_d = din("ctx", [NB, LC, D])
    cT_d = din("cT", [128, 8 * 5])
    wmod_d = [din("wmod%d" % l, [D, 3 * D]) for l in range(2)]
    bmod_d = [din("bmod%d" % l, [128, 24 * 5]) for l in range(2)]
    w_in0_d = din("w_in0", [D, 2560])
    w_out0_d = din("w_out0", [D, D])
    w_in1_d = din("w_in1", [D, 1984])
    w_out1_d = din("w_out1", [D, D])
    poolw_d = din("poolw", [128, 4 * 128])
    pscale_d = din("pscale", [128, 4])
    sguw_d = din("sguw", [128, 4 * 128])
    sgub_d = din("sgub", [1, 4 * 512])
    fnet_d = din("fnet", [512, 512])
    qup_d = din("qup", [256, 1024])
    kvup_d = din("kvup", [128, 1024])
    qn_d = din("qn", [128, 2])
    kvn_d = din("kvn", [128, 1])
    lng_d = [din("lng%d" % l, [1, D]) for l in range(2)]
    lnb_d = [din("lnb%d" % l, [1, D]) for l in range(2)]
    ident_d = din("ident", [128, 128])
    band_d = din("band", [128, 20 * 128])
    cs_d = din("cs128", [128, 256])
    rope_d = din("rope", [128, L])
    dft_d = din("dft", [NT, 128, 2 * NT * 128], BF16)
    out_d = nc.dram_tensor("out", [NB, L, D], F32, kind="ExternalOutput").ap()
    x1_d = nc.dram_tensor("x1s", [NB, L, D], F32, kind="Internal").ap()
    c1_d = nc.dram_tensor("c1s", [NB, LC, D], F32, kind="Internal").ap()
    gate_d = nc.dram_tensor("gates", [2 * 5, D], F32, kind="Internal").ap()
    sfg_d = nc.dram_tensor("sfgs", [128, 4, L], BF16, kind="Internal").ap()
    sdg_d = nc.dram_tensor("sdgs", [128, 4, L], BF16, kind="Internal").ap()
    dbg_d = nc.dram_tensor("dbg_mix", [128, 8, L], BF16, kind="Internal").ap() if DEBUG else None
    dbg2_d = nc.dram_tensor("dbg_kv", [128, 4, L + LC], BF16, kind="Internal").ap() if DEBUG else None

    S = Sched(nc)
    A = Arena(nc, "arena", 206)

    pst = [nc.alloc_psum_tensor("psum%d" % i, [128, 1024], F32) for i in range(4)]
    banks = []
    bank_b = []
    for i in range(4):
        for h in range(2):
            banks.append(pst[i][:, h * 512:(h + 1) * 512])
            bank_b.append(Buf("bank%d" % (2 * i + h)))
    pair_ap = [pst[i][:, :] for i in range(4)]
    rr = {"mm": 0, "tr": 0, "big": 0, "o": 0, "s": 0}
    POOLS = {"mm": [0, 1, 2, 5], "tr": [3, 4], "o": [4, 5, 6, 7], "s": [0, 1, 2, 3]}
    BIG_PAIRS = [3]

    def bank(tag):
        lst = POOLS[tag]
        i = lst[rr[tag] % len(lst)]
        rr[tag] += 1
        return banks[i], bank_b[i]

    def bigpair():
        p = BIG_PAIRS[rr["big"] % len(BIG_PAIRS)]
        rr["big"] += 1
        return pair_ap[p], [bank_b[2 * p], bank_b[2 * p + 1]]

    def MM(out, lhsT, rhs, start, stop, r, w):
        S.op("pe", lambda e: e.matmul(out, lhsT, rhs, start=start, stop=stop), reads=r, writes=w)

    def TR(out, in_, ident, r, w):
        S.op("pe", lambda e: e.transpose(out, in_, ident), reads=r, writes=w)

    def ACT(out, in_, func, r, w, bias=None, scale=None, accum=None):
        kw = {}
        if bias is not None:
            kw["bias"] = bias
        if scale is not None:
            kw["scale"] = scale
        if accum is not None:
            kw["accum_out"] = accum
        S.op("act", lambda e: e.activation(out, in_, func, **kw), reads=r, writes=w)

    def TT(eng, out, in0, in1, op, r, w):
        S.op(eng, lambda e: e.tensor_tensor(out, in0, in1, op), reads=r, writes=w)

    def TS(eng, out, in0, s1, s2, op0, op1, r, w):
        if op1 is None:
            S.op(eng, lambda e: e.tensor_scalar(out, in0, s1, None, op0), reads=r, writes=w)
        else:
            S.op(eng, lambda e: e.tensor_scalar(out, in0, s1, s2, op0, op1), reads=r, writes=w)

    def STT(out, in0, scalar, in1, op0, op1, r, w):
        S.op("dve", lambda e: e.scalar_tensor_tensor(out, in0, scalar, in1, op0, op1), reads=r, writes=w)

    def CP(eng, out, in_, r, w):
        if eng == "act":
            S.op(eng, lambda e: e.activation(out, in_, AF.Copy), reads=r, writes=w)
        else:
            S.op(eng, lambda e: e.tensor_copy(out, in_), reads=r, writes=w)

    ident_f = A.f32(128); ident_fb = Buf("ident_f")
    ident_b = A.bf16(128); ident_bb = Buf("ident_b")
    modT = [A.f32(120) for _ in range(2)]
    modb = [Buf("modT0"), Buf("modT1")]
    poolw = A.bf16(512); poolwb = Buf("poolw")
    pscale = A.f32(4); pscaleb = Buf("pscale")
    sguw = A.bf16(512); sguwb = Buf("sguw")
    bs2 = A.bf16(2048); bs2b = Buf("bs2")
    ones2 = A.bf16(128); ones2b = Buf("ones2")
    fnetw = A.bf16(4 * 512); fnetwb = Buf("fnetw")
    qupw = A.bf16(2 * 1024); qupwb = Buf("qupw")
    kvupw = A.bf16(1024); kvupwb = Buf("kvupw")
    band = A.bf16(20 * 128); bandb = Buf("band")
    cs128 = A.bf16(256); cs128b = Buf("cs128")
    PERS = A.mark()

    S.dma(ident_f, ident_d[:, :], writes=[ident_fb])
    S.dma(ident_b, ident_d[:, :], writes=[ident_bb], queue="pool")
    S.dma(poolw, poolw_d[:, :], writes=[poolwb], queue="pool")
    S.dma(pscale, pscale_d[:, :], writes=[pscaleb])
    S.dma(sguw, sguw_d[:, :], writes=[sguwb], queue="pool")
    S.dma(fnetw.rearrange("p (k n) -> p k n", k=4), fnet_d.rearrange("(k p) n -> p k n", p=128),
          writes=[fnetwb], queue="pool")
    S.dma(band, band_d[:, :], writes=[bandb], queue="pool")
    S.dma(cs128, cs_d[:, :], writes=[cs128b], queue="pool")
    S.op("pool", lambda e: e.memset(ones2, 1.0), writes=[ones2b])

    m0 = A.mark()
    tmpq = A.f32(2 * 1024); tmpqb = Buf("tmpq")
    tmpkv = A.f32(1024); tmpkvb = Buf("tmpkv")
    qn = A.f32(2); qnb = Buf("qn")
    kvn = A.f32(1); kvnb = Buf("kvn")
    sgb_f = A.f32(2048); sgb_fb = Buf("sgb_f")
    sgb_hi = A.bf16(2048); sgb_hib = Buf("sgb_hi")
    sgb_hif = A.f32(2048); sgb_hifb = Buf("sgb_hif")
    sgb_lo = A.bf16(2048); sgb_lob = Buf("sgb_lo")
    S.dma(tmpq.rearrange("p (k n) -> p k n", k=2), qup_d.rearrange("(k p) n -> p k n", p=128), writes=[tmpqb])
    S.dma(tmpkv, kvup_d[:, :], writes=[tmpkvb])
    S.dma(qn, qn_d[:, :], writes=[qnb])
    S.dma(kvn, kvn_d[:, :], writes=[kvnb])
    S.dma(sgb_f[0:1, :], sgub_d[:, :], writes=[sgb_fb])
    for k in range(2):
        TS("dve", qupw[:, k * 1024:(k + 1) * 1024], tmpq[:, k * 1024:(k + 1) * 1024], qn[:, k:k + 1], None,
           ALU.mult, None, [tmpqb, qnb], [qupwb])
    TS("dve", kvupw, tmpkv, kvn[:, 0:1], None, ALU.mult, None, [tmpkvb, kvnb], [kvupwb])
    CP("dve", sgb_hi[0:1, :], sgb_f[0:1, :], [sgb_fb], [sgb_hib])
    CP("dve", sgb_hif[0:1, :], sgb_hi[0:1, :], [sgb_hib], [sgb_hifb])
    TT("dve", sgb_lo[0:1, :], sgb_f[0:1, :], sgb_hif[0:1, :], ALU.subtract, [sgb_fb, sgb_hifb], [sgb_lob])
    S.dma(bs2[0:1, :], sgb_hi[0:1, :], reads=[sgb_hib], writes=[bs2b])
    S.dma(bs2[1:2, :], sgb_lo[0:1, :], reads=[sgb_lob], writes=[bs2b])

    cT = A.f32(40); cTb = Buf("cT")
    scT = A.f32(40); scTb = Buf("scT")
    bmod = A.f32(120); bmodb = Buf("bmod")
    wm = A.f32(8 * 3072); wmb = Buf("wm")
    grow = A.f32(D); growb = Buf("grow")
    S.dma(cT, cT_d[:, :], writes=[cTb])
    ACT(scT, cT, AF.Silu, [cTb], [scTb])
    scT3 = scT.rearrange("p (k j) -> p k j", k=8)
    for l in range(2):
        wm3 = wm.rearrange("p (k n) -> p k n", k=8)
        for k in range(8):
            S.dma(wm3[:, k, :], wmod_d[l][k * 128:(k + 1) * 128, :], writes=[wmb])
        S.dma(bmod, bmod_d[l][:, :], writes=[bmodb])
        pb, pbb = bank("mm")
        for j in range(24):
            for k in range(8):
                MM(pb[:, j * 5:(j + 1) * 5], wm3[:, k, j * 128:(j + 1) * 128], scT3[:, k, :],
                   k == 0, k == 7, [wmb, scTb], [pbb])
        TT("dve", modT[l], pb[:, 0:120], bmod, ALU.add, [pbb, bmodb], [modb[l]])
        TS("dve", modT[l][:, 40:80], modT[l][:, 40:80], 1.0, None, ALU.add, None, [modb[l]], [modb[l]])
        for half in range(2):
            tb, tbb = bank("tr")
            for jj in range(4):
                j = 16 + half * 4 + jj
                TR(tb[0:5, jj * 128:(jj + 1) * 128], modT[l][:, j * 5:(j + 1) * 5], ident_f,
                   [modb[l], ident_fb], [tbb])
            CP("dve", grow[0:5, half * 512:(half + 1) * 512], tb[0:5, :], [tbb], [growb])
        S.dma(gate_d[l * 5:(l + 1) * 5, :], grow[0:5, :], reads=[growb])
    gate_db = Buf("gate_d")
    S.barrier()
    A.reset(m0)

    def mod_ap(l, part, k, ci):
        j = part * 8 + k
        return modT[l][:, j * 5 + ci:j * 5 + ci + 1]

    def out_s1(mixT_tile, mixb, wout, woutb, xres, xresb, gbc, gbcb, work):
        pp, ppb = bigpair()
        for n in range(2):
            for j in range(8):
                MM(pp[:, n * 512:(n + 1) * 512], mixT_tile(j), wout[:, j, n * 512:(n + 1) * 512],
                   j == 0, j == 7, [mixb, woutb], [ppb[n]])
        r = work["r"]; rb = work["rb"]
        TT("dve", r, pp, gbc, ALU.mult, ppb + [gbcb], [rb])
        STT(r, xres, ALPHA, r, ALU.mult, ALU.add, [xresb, rb], [rb])
        st = work["st"]; stb = work["stb"]
        S.op("dve", lambda e: e.bn_stats(st[:, 0:6], r[:, 0:512]), reads=[rb], writes=[stb])
        S.op("dve", lambda e: e.bn_stats(st[:, 6:12], r[:, 512:1024]), reads=[rb], writes=[stb])
        S.op("dve", lambda e: e.bn_aggr(st[:, 12:14], st[:, 0:12]), reads=[stb], writes=[stb])
        ACT(st[:, 14:15], st[:, 13:14], AF.Sqrt, [stb], [stb], bias=EPS, scale=1.0)

    def out_s2(lng, lngb, lnb, lnbb, work, dst_ap):
        r = work["r"]; rb = work["rb"]
        st = work["st"]; stb = work["stb"]
        S.op("dve", lambda e: e.reciprocal(st[:, 15:16], st[:, 14:15]), reads=[stb], writes=[stb])
        STT(st[:, 16:17], st[:, 12:13], -1.0, st[:, 15:16], ALU.mult, ALU.mult, [stb], [stb])
        ACT(r, r, AF.Identity, [rb, stb], [rb], bias=st[:, 16:17], scale=st[:, 15:16])
        TT("pool", r, r, lng, ALU.mult, [rb, lngb], [rb])
        TT("pool", r, r, lnb, ALU.add, [rb, lnbb], [rb])
        S.dma(dst_ap, r, reads=[rb])

    def load_bcast(dst, src_row, dstb):
        S.dma(dst, src_row.partition_broadcast(128), reads=[gate_db], writes=[dstb])

    for b in range(NB):
        A.reset(PERS)
        w_in = A.bf16(8 * 2560); w_inb = Buf("w_in0")
        w_out = A.bf16(8 * 1024); w_outb = Buf("w_out0")
        w_in3 = w_in.rearrange("p (k n) -> p k n", k=8)
        w_out3 = w_out.rearrange("p (k n) -> p k n", k=8)
        for k in range(8):
            S.dma(w_in3[:, k, :], w_in0_d[k * 128:(k + 1) * 128, :], writes=[w_inb], queue="pool")
        for k in range(8):
            S.dma(w_out3[:, k, :], w_out0_d[k * 128:(k + 1) * 128, :], writes=[w_outb], queue="pool")
        lng = A.f32(D); lngb = Buf("lng"); lnb = A.f32(D); lnbb = Buf("lnb")
        gbc = A.f32(D); gbcb = Buf("gbc")
        S.dma(lng, lng_d[0].partition_broadcast(128), writes=[lngb])
        S.dma(lnb, lnb_d[0].partition_broadcast(128), writes=[lnbb])
        xblk = [A.f32(4 * D) for _ in range(2)]; xblkb = [Buf("xblk0"), Buf("xblk1")]
        hT = A.bf16(8 * 512); hTb = Buf("hT")
        a_tok = A.bf16(NT * 512); a_tokb = [Buf("a_tok%d" % t) for t in range(NT)]
        vn = [A.bf16(4 * 512) for _ in range(2)]; vnb = [Buf("vn0"), Buf("vn1")]
        sga = [A.bf16(4 * 512) for _ in range(2)]; sgab = [Buf("sga0"), Buf("sga1")]
        ug = [A.bf16(4 * 512) for _ in range(2)]; ugb = [Buf("ug0"), Buf("ug1")]
        sgbt = A.bf16(4 * 512); sgbtb = Buf("sgbt")
        dT = A.bf16(4 * 512); dTb = Buf("dT")
        mixT = A.bf16(8 * 512); mixTb = Buf("mixT")
        works0 = [{"r": A.f32(D), "rb": Buf("r%d" % i), "st": A.f32(24), "stb": Buf("st%d" % i)} for i in range(2)]
        vst = A.f32(64); vstb = Buf("vst")
        xres = [A.f32(D) for _ in range(2)]; xresb = [Buf("xres0"), Buf("xres1")]
        hT3 = hT.rearrange("p (k n) -> p k n", k=8)
        mixT3 = mixT.rearrange("p (k n) -> p k n", k=8)

        for (src, dst, ntl, ci) in ((ctx_d[b], c1_d[b], NTC, 4), (x_d[b], x1_d[b], NT, b)):
            load_bcast(gbc, gate_d[0 * 5 + ci:0 * 5 + ci + 1, :], gbcb)
            bt = min(4, ntl)
            nblk = ntl // bt
            ntok = bt * 128

            def front_tr(blk):
                s = blk % 2
                xb3 = xblk[s].rearrange("p (t n) -> p t n", t=4)
                for t in range(bt):
                    T = blk * bt + t
                    S.dma(xb3[:, t, :], src[T * 128:(T + 1) * 128, :], writes=[xblkb[s]])
                for k in range(8):
                    tb, tbb = bank("tr")
                    for t in range(bt):
                        TR(tb[:, t * 128:(t + 1) * 128], xb3[:, t, k * 128:(k + 1) * 128], ident_f,
                           [xblkb[s], ident_fb], [tbb])
                    ACT(hT3[:, k, 0:ntok], tb[:, 0:ntok], AF.Identity, [tbb, modb[0]], [hTb],
                        bias=mod_ap(0, 0, k, ci), scale=mod_ap(0, 1, k, ci))

            def front_proj(blk):
                s = blk % 2
                for t in range(bt):
                    T = blk * bt + t
                    pa, pab = bank("mm")
                    for k in range(8):
                        MM(pa, hT3[:, k, t * 128:(t + 1) * 128], w_in3[:, k, 0:512], k == 0, k == 7,
                           [hTb, w_inb], [pab])
                    CP("dve", a_tok[:, T * 512:(T + 1) * 512], pa, [pab], [a_tokb[T]])
                    pv, pvb = bank("mm")
                    for k in range(8):
                        MM(pv, hT3[:, k, t * 128:(t + 1) * 128], w_in3[:, k, 1536:2048], k == 0, k == 7,
                           [hTb, w_inb], [pvb])
                    for hh in range(4):
                        S.op("dve", lambda e, hh=hh, pv=pv: e.bn_stats(vst[:, hh * 6:(hh + 1) * 6], pv[:, hh * 128:(hh + 1) * 128]),
                             reads=[pvb], writes=[vstb])
                    for hh in range(4):
                        S.op("dve", lambda e, hh=hh: e.bn_aggr(vst[:, 24 + hh * 2:26 + hh * 2], vst[:, hh * 6:(hh + 1) * 6]),
                             reads=[vstb], writes=[vstb])
                    var4 = vst[:, 24:32].rearrange("p (h two) -> p h two", two=2)[:, :, 1]
                    mean4 = vst[:, 24:32].rearrange("p (h two) -> p h two", two=2)[:, :, 0]
                    ACT(vst[:, 32:36], var4, AF.Sqrt, [vstb], [vstb], bias=EPS, scale=1.0)
                    S.op("dve", lambda e: e.reciprocal(vst[:, 36:40], vst[:, 32:36]), reads=[vstb], writes=[vstb])
                    STT(vst[:, 40:44], mean4, -1.0, vst[:, 36:40], ALU.mult, ALU.mult, [vstb], [vstb])
                    for hh in range(4):
                        ACT(vn[s][:, t * 512 + hh * 128:t * 512 + (hh + 1) * 128], pv[:, hh * 128:(hh + 1) * 128],
                            AF.Identity, [pvb, vstb], [vnb[s]],
                            bias=vst[:, 40 + hh:41 + hh], scale=vst[:, 36 + hh:37 + hh])
                for jc in range(4):
                    pg, pgb = bank("mm")
                    for k in range(8):
                        MM(pg[:, 0:ntok], w_in3[:, k, 512 + jc * 128:512 + (jc + 1) * 128], hT3[:, k, 0:ntok],
                           k == 0, k == 7, [hTb, w_inb], [pgb])
                    ACT(sga[s][:, jc * 512:jc * 512 + ntok], pg[:, 0:ntok], AF.Silu, [pgb], [sgab[s]])
                for jc in range(4):
                    pg, pgb = bank("mm")
                    for k in range(8):
                        MM(pg[:, 0:ntok], w_in3[:, k, 2048 + jc * 128:2048 + (jc + 1) * 128], hT3[:, k, 0:ntok],
                           k == 0, k == 7, [hTb, w_inb], [pgb])
                    ACT(sgbt[:, jc * 512:jc * 512 + ntok], pg[:, 0:ntok], AF.Silu, [pgb], [sgbtb])
                    pu, pub = bank("mm")
                    for k in range(8):
                        MM(pu[:, 0:ntok], w_in3[:, k, 1024 + jc * 128:1024 + (jc + 1) * 128], hT3[:, k, 0:ntok],
                           k == 0, k == 7, [hTb, w_inb], [pub])
                    TT("dve", ug[s][:, jc * 512:jc * 512 + ntok], pu[:, 0:ntok], sgbt[:, jc * 512:jc * 512 + ntok],
                       ALU.mult, [pub, sgbtb], [ugb[s]])

            def back(blk):
                s = blk % 2
                for g in range(4):
                    pd, pdb = bank("mm")
                    for t in range(bt):
                        T = blk * bt + t
                        srcs = []
                        if T > 0:
                            srcs.append((T - 1, 0))
                        srcs.append((T, 3 if T == 0 else (4 if T == ntl - 1 else 1)))
                        if T < ntl - 1:
                            srcs.append((T + 1, 2))
                        for i, (Ts, kind) in enumerate(srcs):
                            MM(pd[:, t * 128:(t + 1) * 128],
                               a_tok[:, Ts * 512 + g * 128:Ts * 512 + (g + 1) * 128],
                               band[:, (g * 5 + kind) * 128:(g * 5 + kind + 1) * 128],
                               i == 0, i == len(srcs) - 1, [a_tokb[Ts], bandb], [pdb])
                    CP("act", dT[:, g * 512:g * 512 + ntok], pd[:, 0:ntok], [pdb], [dTb])
                for hh in range(4):
                    pf, pfb = bank("mm")
                    MM(pf[:, 0:ntok], ones2[0:2, :], bs2[0:2, hh * 512:hh * 512 + ntok], True, False,
                       [ones2b, bs2b], [pfb])
                    for t in range(bt):
                        MM(pf[:, t * 128:(t + 1) * 128], vn[s][:, t * 512 + hh * 128:t * 512 + (hh + 1) * 128],
                           sguw[:, hh * 128:(hh + 1) * 128], False, t == bt - 1, [vnb[s], sguwb], [pfb])
                    TT("dve", mixT3[:, 4 + hh, 0:ntok], pf[:, 0:ntok], ug[s][:, hh * 512:hh * 512 + ntok], ALU.mult,
                       [pfb, ugb[s]], [mixTb])
                for g in range(4):
                    py, pyb = bank("mm")
                    MM(py[:, 0:ntok], poolw[:, g * 128:(g + 1) * 128], dT[:, g * 512:g * 512 + ntok], True, True,
                       [poolwb, dTb], [pyb])
                    STT(mixT3[:, g, 0:ntok], py[:, 0:ntok], pscale[:, g:g + 1], sga[s][:, g * 512:g * 512 + ntok],
                        ALU.mult, ALU.mult, [pyb, pscaleb, sgab[s]], [mixTb])
                for t in range(bt + 1):
                    T = blk * bt + t
                    if t < bt:
                        xi = T % 2
                        S.dma(xres[xi], src[T * 128:(T + 1) * 128, :], writes=[xresb[xi]])
                        out_s1(lambda j, t=t: mixT3[:, j, t * 128:(t + 1) * 128], mixTb, w_out3, w_outb,
                               xres[xi], xresb[xi], gbc, gbcb, works0[xi])
                    if t >= 1:
                        out_s2(lng, lngb, lnb, lnbb, works0[(T - 1) % 2], dst[(T - 1) * 128:T * 128, :])

            front_tr(0)
            for blk in range(nblk + 1):
                if blk < nblk:
                    front_proj(blk)
                if blk + 1 < nblk:
                    front_tr(blk + 1)
                if blk >= 1:
                    back(blk - 1)
        S.barrier()
        if stop_after_l0:
            continue
        A.reset(PERS)
        mixL = A.bf16(8 * L); mixLb = Buf("mixL")
        cqnT = A.bf16(2 * L); cqnTb = Buf("cqnT")
        ckvnT = A.bf16(L + LC); ckvnTb = Buf("ckvnT")
        krT = A.bf16(L + LC); krTb = Buf("krT")
        rope = A.f32(L); ropeb = Buf("rope")
        lng = A.f32(D); lngb = Buf("lng1"); lnb = A.f32(D); lnbb = Buf("lnb1")
        gbc = A.f32(D); gbcb = Buf("gbc1")
        mixL3 = mixL.rearrange("p (k n) -> p k n", k=8)
        cqnT3 = cqnT.rearrange("p (k n) -> p k n", k=2)
        S.dma(rope, rope_d[:, :], writes=[ropeb])
        S.dma(lng, lng_d[1].partition_broadcast(128), writes=[lngb])
        S.dma(lnb, lnb_d[1].partition_broadcast(128), writes=[lnbb])
        load_bcast(gbc, gate_d[5 + b:5 + b + 1, :], gbcb)
        PL0 = A.mark()
        G = A.bf16(2 * NT * 512); Gb = Buf("G")
        G4 = G.rearrange("p (c l n) -> p c l n", c=2, l=NT)
        PL1 = A.mark()
        w_in1 = A.bf16(8 * 1984); w_in1b = Buf("w_in1")
        w_in13 = w_in1.rearrange("p (k n) -> p k n", k=8)
        for k in range(8):
            S.dma(w_in13[:, k, :], w_in1_d[k * 128:(k + 1) * 128, :], writes=[w_in1b], queue="pool")
        xblk1 = A.f32(4 * D); xblk1b = Buf("xblk1")
        h1T = A.bf16(8 * 512); h1Tb = Buf("h1T")
        h1T3 = h1T.rearrange("p (k n) -> p k n", k=8)
        fT = A.bf16(4 * 512); fTb = Buf("fT")
        gst = [A.bf16(4 * 512) for _ in range(2)]; gstb = [Buf("gst0"), Buf("gst1")]
        nrm = A.bf16(384); nrmb = Buf("nrm")
        krf = A.f32(64); krfb = Buf("krf")
        rp1 = A.f32(128); rp1b = Buf("rp1"); rp2 = A.f32(128); rp2b = Buf("rp2")
        junk = A.bf16(256); junkb = Buf("junk")
        tst = A.f32(8); tstb = Buf("tst")
        xb13 = xblk1.rearrange("p (t n) -> p t n", t=4)

        for (src, ntl, ci, is_ctx, key0) in ((c1_d[b], NTC, 4, True, 0), (x1_d[b], NT, b, False, LC)):
            bt = min(4, ntl)
            nblk = ntl // bt
            ntok = bt * 128
            for blk in range(nblk):
                for t in range(bt):
                    T = blk * bt + t
                    S.dma(xb13[:, t, :], src[T * 128:(T + 1) * 128, :], writes=[xblk1b])
                for k in range(8):
                    tb, tbb = bank("tr")
                    for t in range(bt):
                        TR(tb[:, t * 128:(t + 1) * 128], xb13[:, t, k * 128:(k + 1) * 128], ident_f,
                           [xblk1b, ident_fb], [tbb])
                    ACT(h1T3[:, k, 0:ntok], tb[:, 0:ntok], AF.Identity, [tbb, modb[1]], [h1Tb],
                        bias=mod_ap(1, 0, k, ci), scale=mod_ap(1, 1, k, ci))
                if not is_ctx:
                    for jc in range(4):
                        pg, pgb = bank("mm")
                        for k in range(8):
                            MM(pg, w_in13[:, k, jc * 128:(jc + 1) * 128], h1T3[:, k, :], k == 0, k == 7,
                               [h1Tb, w_in1b], [pgb])
                        CP("dve", fT[:, jc * 512:(jc + 1) * 512], pg, [pgb], [fTb])
                    for gi, (c0, dstg) in enumerate(((512, sfg_d), (1024, sdg_d))):
                        for jc in range(4):
                            pg, pgb = bank("mm")
                            for k in range(8):
                                MM(pg, w_in13[:, k, c0 + jc * 128:c0 + (jc + 1) * 128], h1T3[:, k, :], k == 0, k == 7,
                                   [h1Tb, w_in1b], [pgb])
                            ACT(gst[gi][:, jc * 512:(jc + 1) * 512], pg, AF.Silu, [pgb], [gstb[gi]])
                        S.dma(dstg[:, :, blk * 512:(blk + 1) * 512], gst[gi].rearrange("p (k n) -> p k n", k=4),
                              reads=[gstb[gi]])
                    for t in range(bt):
                        T = blk * bt + t
                        for c in range(2):
                            pg, pgb = bank("mm")
                            for hh in range(4):
                                MM(pg[:, hh * 128:(hh + 1) * 128], fT[:, hh * 512 + t * 128:hh * 512 + (t + 1) * 128],
                                   cs128[:, c * 128:(c + 1) * 128], True, True, [fTb, cs128b], [pgb])
                            CP("act" if c == 0 else "dve", G4[:, c, T, :], pg, [pgb], [Gb])
                c0, c1 = (1792, 1952) if is_ctx else (1536, 1984)
                okv = 0 if is_ctx else 256
                for t in range(bt):
                    T = blk * bt + t
                    pt_, ptb = bank("mm")
                    for k in range(8):
                        MM(pt_[:, 0:c1 - c0], h1T3[:, k, t * 128:(t + 1) * 128], w_in13[:, k, c0:c1], k == 0, k == 7,
                           [h1Tb, w_in1b], [ptb])
                    ACT(junk[:, 0:128], pt_[:, okv:okv + 128], AF.Square, [ptb], [junkb, tstb], accum=tst[:, 1:2])
                    ACT(tst[:, 3:4], tst[:, 1:2], AF.Sqrt, [tstb], [tstb], bias=EPS, scale=1.0 / 128)
                    if not is_ctx:
                        ACT(junk[:, 0:256], pt_[:, 0:256], AF.Square, [ptb], [junkb, tstb], accum=tst[:, 0:1])
                        ACT(tst[:, 2:3], tst[:, 0:1], AF.Sqrt, [tstb], [tstb], bias=EPS, scale=1.0 / 256)
                        S.op("dve", lambda e: e.reciprocal(tst[:, 4:6], tst[:, 2:4]), reads=[tstb], writes=[tstb])
                        TS("dve", nrm[:, 0:256], pt_[:, 0:256], tst[:, 4:5], None, ALU.mult, None, [ptb, tstb], [nrmb])
                    else:
                        S.op("dve", lambda e: e.reciprocal(tst[:, 5:6], tst[:, 3:4]), reads=[tstb], writes=[tstb])
                    TS("dve", nrm[:, 256:384], pt_[:, okv:okv + 128], tst[:, 5:6], None, ALU.mult, None,
                       [ptb, tstb], [nrmb])
                    nkr = 32 if is_ctx else 64
                    CP("dve", krf[:, 0:nkr], pt_[:, okv + 128:okv + 128 + nkr], [ptb], [krfb])
                    tb, tbb = bank("tr")
                    tb16 = tb.bitcast(BF16)
                    j0 = 2 if is_ctx else 0
                    for j in range(j0, 3):
                        TR(tb16[:, j * 128:(j + 1) * 128], nrm[:, j * 128:(j + 1) * 128], ident_b,
                           [nrmb, ident_bb], [tbb])
                    if not is_ctx:
                        CP("act", cqnT3[:, :, T * 128:(T + 1) * 128],
                           tb16[:, 0:256].rearrange("p (k n) -> p k n", k=2), [tbb], [cqnTb])
                    CP("act", ckvnT[:, key0 + T * 128:key0 + (T + 1) * 128], tb16[:, 256:384], [tbb], [ckvnTb])
                    tb2, tb2b = bank("tr")
                    TR(tb2[0:nkr, 0:128], krf[:, 0:nkr], ident_f, [krfb, ident_fb], [tb2b])
                    kcol = krT[0:32, key0 + T * 128:key0 + (T + 1) * 128]
                    if is_ctx:
                        CP("dve", kcol, tb2[0:32, 0:128], [tb2b], [krTb])
                    else:
                        TT("dve", rp1[0:32, :], tb2[0:32, 0:128], rope[0:32, T * 128:(T + 1) * 128], ALU.mult,
                           [tb2b, ropeb], [rp1b])
                        TT("dve", rp2[0:32, :], tb2[32:64, 0:128], rope[32:64, T * 128:(T + 1) * 128], ALU.mult,
                           [tb2b, ropeb], [rp2b])
                        TT("pool", kcol, rp1[0:32, :], rp2[0:32, :], ALU.add, [rp1b, rp2b], [krTb])

        S.barrier()
        A.reset(PL1)
        tab = [A.bf16(2 * NT * 128) for _ in range(2)]; tabb = [Buf("tab0"), Buf("tab1")]
        spec_sb = [A.bf16(512) for _ in range(2)]; spec_sbb = [Buf("spec_sb0"), Buf("spec_sb1")]
        pending_tr = None
        specT = A.bf16(4 * 512); specTb = Buf("specT")
        sfgk = A.bf16(4 * 512); sfgkb = Buf("sfgk")
        specT3 = specT.rearrange("p (h n) -> p h n", h=4)
        sfgk3 = sfgk.rearrange("p (h n) -> p h n", h=4)
        fnetw3 = fnetw.rearrange("p (k n) -> p k n", k=4)
        for blk in range(4):
            S.dma(sfgk3, sfg_d[:, :, blk * 512:(blk + 1) * 512], writes=[sfgkb])
            for q4 in range(4):
                kt = blk * 4 + q4
                tbuf = tab[kt % 2]
                S.dma(tbuf, dft_d[kt], writes=[tabb[kt % 2]])
                tab4 = tbuf.rearrange("p (c l k) -> p c l k", c=2, l=NT)
                ps_, psb_ = bank("mm")
                i = 0
                for c in range(2):
                    for lt in range(NT):
                        MM(ps_, tab4[:, c, lt, :], G4[:, c, lt, :], i == 0, i == 2 * NT - 1,
                           [tabb[kt % 2], Gb], [psb_])
                        i += 1
                if pending_tr is not None:
                    pending_tr()
                sp_i = kt % 2
                CP("act", spec_sb[sp_i], ps_, [psb_], [spec_sbb[sp_i]])

                def do_tr(sp_i=sp_i, q4=q4):
                    tb, tbb = bank("tr")
                    tb16 = tb.bitcast(BF16)
                    for hh in range(4):
                        TR(tb16[:, hh * 128:(hh + 1) * 128], spec_sb[sp_i][:, hh * 128:(hh + 1) * 128], ident_b,
                           [spec_sbb[sp_i], ident_bb], [tbb])
                    CP("dve", specT3[:, :, q4 * 128:(q4 + 1) * 128],
                       tb16[:, 0:512].rearrange("p (h n) -> p h n", h=4), [tbb], [specTb])
                pending_tr = do_tr
                if q4 == 3:
                    pending_tr()
                    pending_tr = None
            for j in range(4):
                pf, pfb = bank("mm")
                for hh in range(4):
                    MM(pf, fnetw3[:, hh, j * 128:(j + 1) * 128], specT3[:, hh, :], hh == 0, hh == 3,
                       [fnetwb, specTb], [pfb])
                TT("dve", mixL3[:, j, blk * 512:(blk + 1) * 512], pf, sfgk3[:, j, :], ALU.mult,
                   [pfb, sfgkb], [mixLb])

        S.barrier()
        A.reset(PL0)
        w_out1 = A.bf16(8 * 1024); w_out1b = Buf("w_out1")
        w_out13 = w_out1.rearrange("p (k n) -> p k n", k=8)
        for k in range(8):
            S.dma(w_out13[:, k, :], w_out1_d[k * 128:(k + 1) * 128, :], writes=[w_out1b], queue="pool")
        PL0b = A.mark()
        Vp = A.bf16(NKT * 768); Vpb = Buf("Vp")
        Vp4 = Vp.rearrange("p (t g n) -> p t g n", t=NKT, g=4)
        kT = [A.bf16(L + LC) for _ in range(2)]; kTb = [Buf("kT0"), Buf("kT1")]
        qT = [A.bf16(L) for _ in range(2)]; qTb = [Buf("qT0"), Buf("qT1")]
        PT = [A.bf16(512) for _ in range(4)]; PTb = [Buf("PT%d" % i) for i in range(4)]
        tmp1 = A.f32(512); tmp1b = Buf("tmp1"); tmp2 = A.f32(512); tmp2b = Buf("tmp2")
        rec = [A.f32(512) for _ in range(2)]; recb = [Buf("rec0"), Buf("rec1")]
        sdgk = [A.bf16(512) for _ in range(2)]; sdgkb = [Buf("sdgk0"), Buf("sdgk1")]
        qupw3 = qupw.rearrange("p (k n) -> p k n", k=2)
        S.op("pool", lambda e, o_=Vp: e.memset(o_, 1.0), writes=[Vpb])
        for kt in range(NKT):
            pv, pvb = bank("mm")
            MM(pv, ckvnT[:, kt * 128:(kt + 1) * 128], kvupw[:, 512:1024], True, True, [ckvnTb, kvupwb], [pvb])
            pv4 = pv.rearrange("p (g two n) -> p g two n", g=4, two=2)
            CP("dve", Vp4[:, kt, :, 0:64], pv4[:, :, 0, :], [pvb], [Vpb])
            CP("act", Vp4[:, kt, :, 128:192], pv4[:, :, 1, :], [pvb], [Vpb])
        nrec = [0]

        def head_proj(hh):
            s = hh % 2
            for (c0, n) in ((0, 512), (512, 512), (1024, 512), (1536, 512), (2048, 256)):
                pk, pkb = bank("o")
                MM(pk[0:64, 0:n], kvupw[:, hh * 64:(hh + 1) * 64], ckvnT[:, c0:c0 + n], True, True,
                   [kvupwb, ckvnTb], [pkb])
                CP("dve", kT[s][0:64, c0:c0 + n], pk[0:64, 0:n], [pkb], [kTb[s]])
            CP("pool", kT[s][64:96, :], krT[0:32, :], [krTb], [kTb[s]])
            for qb in range(4):
                pq, pqb = bank("o")
                pp_, ppb_ = bank("o")
                for kc in range(2):
                    MM(pq[0:96, :], qupw3[:, kc, hh * 96:(hh + 1) * 96], cqnT3[:, kc, qb * 512:(qb + 1) * 512],
                       kc == 0, kc == 1, [qupwb, cqnTb], [pqb])
                for kc in range(2):
                    MM(pp_[0:32, :], qupw3[:, kc, 768 + hh * 32:768 + (hh + 1) * 32],
                       cqnT3[:, kc, qb * 512:(qb + 1) * 512], kc == 0, kc == 1, [qupwb, cqnTb], [ppb_])
                CP("act", qT[s][0:64, qb * 512:(qb + 1) * 512], pq[0:64, :], [pqb], [qTb[s]])
                TT("dve", tmp1[64:96, :], pq[64:96, :], rope[64:96, qb * 512:(qb + 1) * 512], ALU.mult,
                   [pqb, ropeb], [tmp1b])
                TT("dve", tmp2[64:96, :], pp_[0:32, :], rope[32:64, qb * 512:(qb + 1) * 512], ALU.mult,
                   [ppb_, ropeb], [tmp2b])
                TT("pool", qT[s][64:96, qb * 512:(qb + 1) * 512], tmp1[64:96, :], tmp2[64:96, :], ALU.add,
                   [tmp1b, tmp2b], [qTb[s]])

        def head_attn(hh):
            s = hh % 2
            pr = hh // 2
            a0 = (hh % 2) * 64
            b0 = 64 - a0
            for qb in range(4):
                po, pob = bank("o")

                def score(kt):
                    psc, pscb = bank("s")
                    MM(psc, kT[s][0:96, kt * 128:(kt + 1) * 128], qT[s][0:96, qb * 512:(qb + 1) * 512], True, True,
                       [kTb[s], qTb[s]], [pscb])
                    return psc, pscb

                pend = [score(0), score(1), score(2)]
                for kt in range(NKT):
                    psc, pscb = pend.pop(0)
                    pi = (hh * 4 * NKT + qb * NKT + kt) % 4
                    ACT(PT[pi], psc, AF.Exp, [pscb], [PTb[pi]], scale=SCALE)
                    if kt + 3 < NKT:
                        pend.append(score(kt + 3))
                    lv = Vp4[:, kt, pr, 0:128] if a0 == 0 else Vp4[:, kt, pr, 64:192]
                    MM(po, lv, PT[pi], kt == 0, kt == NKT - 1, [Vpb, PTb[pi]], [pob])
                ri = nrec[0] % 2
                nrec[0] += 1
                S.op("dve", lambda e, o_=rec[ri][a0:a0 + 64, :], i_=po[b0:b0 + 64, :]: e.reciprocal(o_, i_),
                     reads=[pob], writes=[recb[ri]])
                S.dma(sdgk[ri][a0:a0 + 64, :], sdg_d[a0:a0 + 64, pr, qb * 512:(qb + 1) * 512], writes=[sdgkb[ri]])
                TT("pool", rec[ri][a0:a0 + 64, :], rec[ri][a0:a0 + 64, :], sdgk[ri][a0:a0 + 64, :], ALU.mult,
                   [recb[ri], sdgkb[ri]], [recb[ri]])
                TT("dve", mixL3[a0:a0 + 64, 4 + pr, qb * 512:(qb + 1) * 512], po[a0:a0 + 64, :],
                   rec[ri][a0:a0 + 64, :], ALU.mult, [pob, recb[ri]], [mixLb])

        head_proj(0)
        for hh in range(8):
            if hh + 1 < 8:
                head_proj(hh + 1)
            head_attn(hh)

        S.barrier()
        if DEBUG and b == 0:
            S.dma(dbg_d, mixL3, reads=[mixLb])
            S.dma(dbg2_d[:, 0, :], ckvnT, reads=[ckvnTb])
            S.dma(dbg2_d[:, 1, :], krT, reads=[krTb])
            S.dma(dbg2_d[:, 2, 0:L], cqnT3[:, 0, :], reads=[cqnTb])
            S.dma(dbg2_d[:, 3, 0:L], cqnT3[:, 1, :], reads=[cqnTb])
            S.barrier()
        A.reset(PL0b)
        xt = [A.f32(D) for _ in range(3)]; xtb = [Buf("xt%d" % i) for i in range(3)]
        works = [{"r": A.f32(D), "rb": Buf("r%d" % i), "st": A.f32(24), "stb": Buf("st%d" % i)} for i in range(3)]
        BIG_PAIRS[:] = [2, 3]
        for T in range(NT + 2):
            if T < NT:
                S.dma(xt[T % 3], x1_d[b][T * 128:(T + 1) * 128, :], writes=[xtb[T % 3]])
                out_s1(lambda j, T=T: mixL3[:, j, T * 128:(T + 1) * 128], mixLb, w_out13, w_out1b,
                       xt[T % 3], xtb[T % 3], gbc, gbcb, works[T % 3])
            if T >= 2:
                out_s2(lng, lngb, lnb, lnbb, works[(T - 2) % 3], out_d[b][(T - 2) * 128:(T - 1) * 128, :])
        BIG_PAIRS[:] = [3]
        S.barrier()

    if stop_after_l0:
        A.reset(PERS)
        tb_ = A.f32(D); tbb_ = Buf("cpy")
        for b in range(NB):
            for T in range(NT):
                S.dma(tb_, x1_d[b][T * 128:(T + 1) * 128, :], writes=[tbb_])
                S.dma(out_d[b][T * 128:(T + 1) * 128, :], tb_, reads=[tbb_])

    S.finish()
    S.replay()
    return nc, S


def _band_mats():
    Ls = 384
    out = np.zeros((4, 5, 128, 128), np.float32)
    for g, w in enumerate((2, 4, 8, 16)):
        P = np.zeros((Ls, Ls), np.float64)
        for t in range(Ls):
            lo = max(t - w // 2, 0)
            hi = min(t + w - w // 2, Ls)
            P[t, lo:hi] = 1.0 / (hi - lo)
        M = P - np.eye(Ls)
        blk = lambda to, ti: M[to * 128:(to + 1) * 128, ti * 128:(ti + 1) * 128].T
        out[g, 0] = blk(1, 0)
        out[g, 1] = blk(1, 1)
        out[g, 2] = blk(1, 2)
        out[g, 3] = blk(0, 0)
        out[g, 4] = blk(2, 2)
    return np.ascontiguousarray(out.transpose(2, 0, 1, 3).reshape(128, 20 * 128))


def _perm32():
    p = np.zeros(32, np.int64)
    for d in range(32):
        base = (d // 16) * 16
        p[d] = base + ((d % 16) + 8) % 16
    return p


def _consts():
    c = {}
    c["ident"] = np.eye(128, dtype=np.float32)
    c["band"] = _band_mats()
    cc = np.arange(128, dtype=np.float64)
    ang = 2.0 * np.pi * np.outer(cc, cc) / 128.0
    c["cs128"] = np.concatenate([np.cos(ang), np.sin(ang)], axis=1).astype(np.float32) / np.float32(np.sqrt(128.0))
    pos = np.arange(L)
    row = (pos // 64).astype(np.float32)
    col = (pos % 64).astype(np.float32)
    inv = (np.float32(10000.0) ** (-np.arange(0, 16, 2, dtype=np.float32) / np.float32(16.0))).astype(np.float32)
    ang_r = (row[:, None] * inv[None, :]).astype(np.float32).astype(np.float64)
    ang_c = (col[:, None] * inv[None, :]).astype(np.float32).astype(np.float64)
    cosT = np.zeros((32, L)); sinT = np.zeros((32, L))
    for d in range(32):
        a = ang_r if d < 16 else ang_c
        i8 = (d % 16) % 8
        cosT[d] = np.cos(a[:, i8])
        sgn = -1.0 if (d % 16) < 8 else 1.0
        sinT[d] = sgn * np.sin(a[:, i8])
    c["rope"] = np.concatenate([cosT, sinT, cosT, sinT], axis=0).astype(np.float32)
    l_idx = (np.arange(NT)[None, :] * 128 + np.arange(128)[:, None])
    k_idx = np.arange(L).reshape(NT, 128)
    prod = (l_idx[None, :, :, None].astype(np.int64) * k_idx[:, None, None, :].astype(np.int64)) % L
    th = 2.0 * np.pi * prod.astype(np.float64) / L
    tab = np.stack([np.cos(th), -np.sin(th)], axis=2) / np.sqrt(float(L))
    c["dft"] = np.ascontiguousarray(tab.reshape(NT, 128, 2 * NT * 128).astype(np.float32)).astype(ml_dtypes.bfloat16)
    return c


_CONSTS = None


def _get_consts():
    global _CONSTS
    if _CONSTS is None:
        _CONSTS = _consts()
    return _CONSTS


def _host_maps(inp):
    f = lambda a: np.ascontiguousarray(np.asarray(a, dtype=np.float32))
    cst = _get_consts()
    shared = dict(cst)
    for l, pre in enumerate(("ab", "cd")):
        shared["wmod%d" % l] = f(inp[pre + "_w_mod"][0])
        bm = f(inp[pre + "_b_mod"][0]).reshape(24, 128).T
        shared["bmod%d" % l] = f(np.repeat(bm[:, :, None], 5, axis=2).reshape(128, 120))
        shared["lng%d" % l] = f(inp[pre + "_ln_g"][0]).reshape(1, D)
        shared["lnb%d" % l] = f(inp[pre + "_ln_b"][0]).reshape(1, D)
    shared["w_in0"] = f(inp["ab_w_in"][0])
    shared["w_out0"] = f(inp["ab_w_out"][0])
    w1 = f(inp["cd_w_in"][0])
    p32 = _perm32()
    kr = w1[:, 1408:1440]
    shared["w_in1"] = f(np.concatenate([w1[:, 0:512], w1[:, 512:1024], w1[:, 1440:1952], w1[:, 1024:1280],
                                        w1[:, 1280:1408], kr, kr[:, p32]], axis=1))
    shared["w_out1"] = f(inp["cd_w_out"][0])
    shared["poolw"] = f(f(inp["ab_pool_w"][0]).transpose(1, 0, 2).reshape(128, 512))
    shared["pscale"] = f(f(inp["ab_pool_scale"][0]).reshape(4, 128).T)
    shared["sguw"] = f(f(inp["ab_sgu_w"][0]).transpose(2, 0, 1).reshape(128, 512))
    sb = f(inp["ab_sgu_b"][0])
    shared["sgub"] = f(np.repeat(sb[:, None, :], 4, axis=1).reshape(1, 2048))
    shared["fnet"] = f(inp["cd_fnet_w"][0])
    qu = f(inp["cd_w_q_up"][0]).reshape(256, 8, 96)
    shared["qup"] = f(np.concatenate([qu.reshape(256, 768), qu[:, :, 64:96][:, :, p32].reshape(256, 256)], axis=1))
    kv = f(inp["cd_w_kv_up"][0]).reshape(128, 8, 128)
    shared["kvup"] = f(np.concatenate([kv[:, :, 0:64].reshape(128, 512), kv[:, :, 64:128].reshape(128, 512)], axis=1))
    shared["qn"] = f(f(inp["cd_q_norm"][0]).reshape(2, 128).T)
    shared["kvn"] = f(f(inp["cd_kv_norm"][0]).reshape(128, 1))
    x = np.asarray(inp["x"], dtype=np.float32)
    ctx = np.asarray(inp["ctx"], dtype=np.float32)
    c = f(inp["c"])
    c_ctx = f(inp["c_ctx"]).reshape(1, D)
    maps = []
    for i in range(NCORES):
        m = dict(shared)
        m["x"] = np.ascontiguousarray(x[i * NB:(i + 1) * NB])
        m["ctx"] = np.ascontiguousarray(ctx[i * NB:(i + 1) * NB])
        cond = np.concatenate([c[i * NB:(i + 1) * NB], c_ctx], axis=0)
        m["cT"] = f(cond.reshape(5, 8, 128).transpose(2, 1, 0).reshape(128, 40))
        maps.append(m)
    return maps


_PROG = {}


def _get_prog(stop_after_l0=False):
    if stop_after_l0 not in _PROG:
        _PROG[stop_after_l0] = build_program(stop_after_l0)[0]
    return _PROG[stop_after_l0]


def kernel(**inp):
    maps = _host_maps(inp)
    nc = _get_prog(False)
    res = run_bass_kernel_spmd(nc, maps, core_ids=list(range(NCORES)))
    return np.concatenate([np.asarray(r["out"], dtype=np.float32) for r in res.results], axis=0)
```
